# Optimizing a Trainium2 kernel written in Bass

```python
import math
import jax
import jax.numpy as jnp
from jax import lax
import numpy as np

D_MODEL = 1024
BATCH = 2
SEQ = 8192
DEPTH = 1

MIX_WIDTH = D_MODEL
FOX_WIDTH = MIX_WIDTH // 2
FOX_HEAD_DIM = 64
FOX_HEADS = FOX_WIDTH // FOX_HEAD_DIM
GLA_WIDTH = MIX_WIDTH - FOX_WIDTH
GLA_HEADS = 4
GLA_VAL_DIM = GLA_WIDTH // GLA_HEADS
GLA_KEY_DIM = GLA_VAL_DIM // 2
GLA_KEY_WIDTH = GLA_HEADS * GLA_KEY_DIM
GLA_GATE_RANK = 16
GLA_GATE_TEMP = 16.0
GLA_CHUNK = 64
Q_BLOCK = 128
D_FF = ((8 * D_MODEL // 3 + 255) // 256) * 256
LN_EPS = 1e-5
RMS_EPS = 1e-6
N_MOD = 6

DEEPNORM_ALPHA = (2.0 * DEPTH) ** 0.25
DEEPNORM_BETA = (8.0 * DEPTH) ** -0.25

IN_SPLITS = [
    FOX_WIDTH,
    FOX_WIDTH,
    FOX_WIDTH,
    FOX_HEADS,
    GLA_KEY_WIDTH,
    GLA_KEY_WIDTH,
    GLA_WIDTH,
    GLA_GATE_RANK,
    GLA_WIDTH,
]
IN_WIDTH = sum(IN_SPLITS)

kernel_name = "hymba_fox_gla_deepnorm_adaln_layer"


def layer_norm(x, g, b):
    xf = x.astype(jnp.float32)
    mu = jnp.mean(xf, axis=-1, keepdims=True)
    var = jnp.mean(jnp.square(xf - mu), axis=-1, keepdims=True)
    y = (xf - mu) * lax.rsqrt(var + LN_EPS)
    return (y * g.astype(jnp.float32) + b.astype(jnp.float32)).astype(x.dtype)


def fox_attention(q, k, v, f_logit, b_f):
    B, S, H, Dh = q.shape
    scale = 1.0 / math.sqrt(Dh)
    log_f = jax.nn.log_sigmoid((f_logit + b_f).astype(jnp.float32))
    dcum = jnp.cumsum(log_f, axis=1).transpose(0, 2, 1)
    qh = q.transpose(0, 2, 1, 3)
    kh = k.transpose(0, 2, 1, 3)
    vh = v.transpose(0, 2, 1, 3)
    nq = S // Q_BLOCK
    q_blocks = qh.reshape(B, H, nq, Q_BLOCK, Dh).transpose(2, 0, 1, 3, 4)
    d_blocks = dcum.reshape(B, H, nq, Q_BLOCK).transpose(2, 0, 1, 3)
    k_pos = jnp.arange(S)

    def one_block(args):
        qb, dqb, i = args
        s = jnp.einsum('bhqd,bhkd->bhqk', qb, kh).astype(jnp.float32) * scale
        s = s + dqb[..., None] - dcum[:, :, None, :]
        q_pos = i * Q_BLOCK + jnp.arange(Q_BLOCK)
        s = jnp.where(k_pos[None, :] <= q_pos[:, None], s, -1e30)
        p = jax.nn.softmax(s, axis=-1).astype(vh.dtype)
        return jnp.einsum('bhqk,bhkd->bhqd', p, vh)

    o = lax.map(one_block, (q_blocks, d_blocks, jnp.arange(nq)))
    return o.transpose(1, 0, 3, 2, 4).reshape(B, S, H * Dh)


def gla_attention(q, k, v, a_low, w_a2, b_a, r, g_norm):
    B, S, H, dk = q.shape
    dv = v.shape[-1]
    out_dtype = v.dtype
    C = GLA_CHUNK
    nc = S // C
    log_a = jax.nn.log_sigmoid((a_low @ w_a2 + b_a).astype(jnp.float32)) / GLA_GATE_TEMP
    log_a = log_a.reshape(B, S, H, dk)

    def chunks(t):
        return t.astype(jnp.float32).reshape(B, nc, C, H, -1).transpose(0, 3, 1, 2, 4)

    qc = chunks(q) * (dk ** -0.5)
    kc = chunks(k)
    vc = chunks(v)
    b = jnp.cumsum(chunks(log_a), axis=3)
    b_last = b[:, :, :, -1:, :]
    q_dec = qc * jnp.exp(b)
    k_inv = kc * jnp.exp(-b)
    k_to_end = kc * jnp.exp(b_last - b)
    causal = jnp.tril(jnp.ones((C, C), dtype=bool))
    a_intra = jnp.einsum('bhncd,bhnsd->bhncs', q_dec, k_inv)
    a_intra = jnp.where(causal, a_intra, 0.0)
    o_intra = jnp.einsum('bhncs,bhnse->bhnce', a_intra, vc)
    kv_chunk = jnp.einsum('bhncd,bhnce->bhnde', k_to_end, vc)
    decay = jnp.exp(b_last[:, :, :, 0, :])

    def step(state, inp):
        dec, kv = inp
        return dec[..., None] * state + kv, state

    init = jnp.zeros((B, H, dk, dv), jnp.float32)
    _, states = lax.scan(step, init, (decay.transpose(2, 0, 1, 3), kv_chunk.transpose(2, 0, 1, 3, 4)))
    states = states.transpose(1, 2, 0, 3, 4)
    o_inter = jnp.einsum('bhncd,bhnde->bhnce', q_dec, states)
    o = (o_intra + o_inter).transpose(0, 2, 3, 1, 4).reshape(B, S, H, dv)
    o = o * lax.rsqrt(jnp.mean(jnp.square(o), axis=-1, keepdims=True) + RMS_EPS)
    o = o.reshape(B, S, H * dv) * g_norm.astype(jnp.float32)
    return (o * jax.nn.silu(r.astype(jnp.float32))).astype(out_dtype)


def setup_inputs(seed: int = 0) -> dict:
    key = jax.random.key(seed)
    ks = jax.random.split(key, 17)
    f32 = jnp.float32
    nrm = lambda k, shape, s: (jax.random.normal(k, shape, f32) * s)
    return {
        "x": nrm(ks[0], (BATCH, SEQ, D_MODEL), 1.0),
        "c": nrm(ks[1], (BATCH, D_MODEL), 1.0),
        "w_c": nrm(ks[2], (D_MODEL, N_MOD * D_MODEL), 0.3 * D_MODEL ** -0.5),
        "b_c": nrm(ks[3], (N_MOD * D_MODEL,), 0.02),
        "w_in": nrm(ks[4], (D_MODEL, IN_WIDTH), D_MODEL ** -0.5),
        "b_f": nrm(ks[5], (FOX_HEADS,), 0.1),
        "w_a2": nrm(ks[6], (GLA_GATE_RANK, GLA_KEY_WIDTH), GLA_GATE_RANK ** -0.5),
        "b_a": nrm(ks[7], (GLA_KEY_WIDTH,), 0.1),
        "g_gla": 1.0 + nrm(ks[8], (GLA_WIDTH,), 0.02),
        "w_o": nrm(ks[9], (MIX_WIDTH, D_MODEL), DEEPNORM_BETA * MIX_WIDTH ** -0.5),
        "ln1_g": 1.0 + nrm(ks[10], (D_MODEL,), 0.02),
        "ln1_b": nrm(ks[11], (D_MODEL,), 0.02),
        "w_gate": nrm(ks[12], (D_MODEL, D_FF), D_MODEL ** -0.5),
        "w_up": nrm(ks[13], (D_MODEL, D_FF), D_MODEL ** -0.5),
        "w_down": nrm(ks[14], (D_FF, D_MODEL), DEEPNORM_BETA * D_FF ** -0.5),
        "ln2_g": 1.0 + nrm(ks[15], (D_MODEL,), 0.02),
        "ln2_b": nrm(ks[16], (D_MODEL,), 0.02),
    }


def reference(x, c, w_c, b_c, w_in, b_f, w_a2, b_a, g_gla, w_o, ln1_g, ln1_b,
              w_gate, w_up, w_down, ln2_g, ln2_b):
    B, S, D = x.shape
    mod = (c @ w_c + b_c).reshape(B, N_MOD, D)
    shift_m, scale_m, gate_m = mod[:, 0, None, :], mod[:, 1, None, :], mod[:, 2, None, :]
    shift_f, scale_f, gate_f = mod[:, 3, None, :], mod[:, 4, None, :], mod[:, 5, None, :]
    split_idx = list(np.cumsum(IN_SPLITS)[:-1])

    for _ in range(DEPTH):
        u = x * (1.0 + scale_m) + shift_m
        proj = u @ w_in
        fq, fk, fv, ff, gq, gk, gv, ga, gr = jnp.split(proj, split_idx, axis=-1)
        fox_out = fox_attention(
            fq.reshape(B, S, FOX_HEADS, FOX_HEAD_DIM),
            fk.reshape(B, S, FOX_HEADS, FOX_HEAD_DIM),
            fv.reshape(B, S, FOX_HEADS, FOX_HEAD_DIM),
            ff, b_f)
        gla_out = gla_attention(
            gq.reshape(B, S, GLA_HEADS, GLA_KEY_DIM),
            gk.reshape(B, S, GLA_HEADS, GLA_KEY_DIM),
            gv.reshape(B, S, GLA_HEADS, GLA_VAL_DIM),
            ga, w_a2, b_a, gr, g_gla)
        y = jnp.concatenate([fox_out, gla_out], axis=-1) @ w_o
        x = layer_norm(DEEPNORM_ALPHA * x + (1.0 + gate_m) * y, ln1_g, ln1_b)

        u2 = x * (1.0 + scale_f) + shift_f
        h = jax.nn.silu(u2 @ w_gate) * (u2 @ w_up)
        y2 = h @ w_down
        x = layer_norm(DEEPNORM_ALPHA * x + (1.0 + gate_f) * y2, ln2_g, ln2_b)
    return x
```

```python
import numpy as np
import ml_dtypes
from contextlib import ExitStack

import concourse.bass as bass
import concourse.mybir as mybir
from concourse.bass_utils import run_bass_kernel_spmd

F32 = mybir.dt.float32
BF16 = mybir.dt.bfloat16
AF = mybir.ActivationFunctionType
ALU = mybir.AluOpType

D = 1024
SEQ = 8192
OWN = 2048
NREST = SEQ - OWN
DFF = 2816
NFC = DFF // 128
ALPHA = 2.0 ** 0.25
LN_EPS = 1e-5
RMS_EPS = 1e-6
NEG = -30000.0
C_FQ, C_FK, C_FV, C_FF, C_GQ, C_GK, C_GV, C_GA, C_GR = 0, 512, 1024, 1536, 1544, 1800, 2056, 2568, 2584
INW = 3096

_DEBUG = None


class Buf:
    __slots__ = ("name", "lw", "rd", "dsem", "dcnt")

    def __init__(self, name):
        self.name = name
        self.lw = None
        self.rd = {}
        self.dsem = None
        self.dcnt = 0


class Tile:
    def __init__(self, t, name, whole=False):
        self.t = t
        self.name = name
        self.b = Buf(name)
        self._subs = {}
        self.whole = whole

    def s(self, key):
        if self.whole:
            return self.b
        if key not in self._subs:
            self._subs[key] = Buf("%s.%s" % (self.name, key))
        return self._subs[key]


ENGS = ("sync", "scalar", "vector", "gpsimd", "tensor")


class _Rec:
    def __init__(self):
        self.call = None

    def __getattr__(self, name):
        def f(*a, **kw):
            self.call = (name, a, kw)
            return self
        return f


class Prog:
    def __init__(self, nc, stack):
        self.nc = nc
        self.stack = stack
        self.q = {e: [] for e in ENGS}
        self.sem = {e: stack.enter_context(nc.semaphore("s_" + e)) for e in ENGS}
        self.cnt = {e: 0 for e in ENGS}
        self.seen = {e: {} for e in ENGS}
        self.dma_bufs = []
        self.nblock = 0
        self.rr = 0

    def _tokcount(self, tok):
        kind, owner, c = tok
        if kind == "d":
            return owner.dcnt
        return c

    def _deps(self, reads, writes):
        deps = {}

        def add(tok):
            if tok is None:
                return
            kind, owner, c = tok
            key = ("d", owner.name) if kind == "d" else ("e", owner)
            c = self._tokcount(tok)
            if key not in deps or deps[key][1] < c:
                deps[key] = (tok, c)

        for b in reads:
            add(b.lw)
        for b in writes:
            add(b.lw)
            for tok in b.rd.values():
                add(tok)
        return deps

    def _emit_waits(self, eng, deps):
        for key, (tok, c) in deps.items():
            kind, owner, _ = tok
            if kind == "e" and owner == eng and eng == "tensor":
                continue
            if self.seen[eng].get(key, 0) >= c:
                continue
            self.seen[eng][key] = c
            h = owner.dsem if kind == "d" else self.sem[owner]
            self.q[eng].append(("wait", h, c))

    def op(self, eng, fn, reads=(), writes=(), inc=True):
        deps = self._deps(reads, writes)
        self._emit_waits(eng, deps)
        if inc:
            self.cnt[eng] += 1
            c = self.cnt[eng]
        else:
            assert eng == "tensor"
            c = self.cnt[eng] + 1
        rec = _Rec()
        fn(rec)
        assert rec.call is not None
        self.q[eng].append(("op", rec.call, inc))
        tok = ("e", eng, c)
        for b in writes:
            b.lw = tok
            b.rd = {}
        for b in reads:
            b.rd[("e", eng)] = tok

    def dma(self, q, out, in_, reads=(), writes=(), **kw):
        deps = self._deps(reads, writes)
        self._emit_waits(q, deps)
        b = writes[0]
        if b.dsem is None:
            b.dsem = self.stack.enter_context(self.nc.semaphore("d%d" % len(self.dma_bufs)))
            self.dma_bufs.append(b)
        b.dcnt += 16
        tok = ("d", b, b.dcnt)
        self.q[q].append(("dma", out, in_, b.dsem, kw))
        for w in writes:
            w.lw = tok
            w.rd = {}
        for r in reads:
            r.rd[("d", b.name)] = tok

    def mm(self, out, lhsT, rhs, start, stop, reads, writes, inc=None):
        if inc is None:
            inc = stop
        self.op("tensor", lambda e, o=out, l=lhsT, r=rhs, a=start, z=stop: e.matmul(o, l, r, start=a, stop=z),
                reads=reads, writes=writes, inc=inc)

    def tr(self, out, in_, ident, reads, writes, inc=True):
        self.op("tensor", lambda e, o=out, i=in_, d=ident: e.transpose(o, i, d), reads=reads, writes=writes, inc=inc)

    def act(self, out, in_, func, reads, writes, bias=None, scale=None):
        kw = {}
        if bias is not None:
            kw["bias"] = bias
        if scale is not None:
            kw["scale"] = scale
        self.op("scalar", lambda e, o=out, i=in_, f=func, k=kw: e.activation(o, i, f, **k), reads=reads, writes=writes)

    def finish_phase(self):
        nc = self.nc
        for b in self.dma_bufs:
            key = ("d", b.name)
            if self.seen["sync"].get(key, 0) < b.dcnt:
                self.seen["sync"][key] = b.dcnt
                self.q["sync"].append(("wait", b.dsem, b.dcnt))
        for e in ENGS:
            if e == "sync":
                continue
            key = ("e", e)
            if self.cnt[e] > 0 and self.seen["sync"].get(key, 0) < self.cnt[e]:
                self.seen["sync"][key] = self.cnt[e]
                self.q["sync"].append(("wait", self.sem[e], self.cnt[e]))
        q = self.q
        sem = self.sem

        def replay(e, name):
            for it in q[name]:
                if it[0] == "wait":
                    e.wait_ge(it[1], it[2])
                elif it[0] == "op":
                    nm, a, kw_ = it[1]
                    ins = getattr(e, nm)(*a, **kw_)
                    if it[2]:
                        ins.then_inc(sem[name], 1)
                else:
                    _, out, in_, dsem, kw = it
                    e.dma_start(out=out, in_=in_, **kw).then_inc(dsem, 16)

        with nc.Block() as block:
            @block.sync
            def _(e):
                replay(e, "sync")

            @block.scalar
            def _(e):
                replay(e, "scalar")

            @block.vector
            def _(e):
                replay(e, "vector")

            @block.gpsimd
            def _(e):
                replay(e, "gpsimd")

            @block.tensor
            def _(e):
                replay(e, "tensor")
        self.q = {e: [] for e in ENGS}
        self.nblock += 1


def build(debug=None):
    debug = debug or {}
    stop_after = debug.get("stop")
    nc = bass.Bass("TRN2", target_bir_lowering=False)

    declared = []
    early = stop_after in ("p0", "pA", "pF")

    def din(name, shape, dt=F32):
        if early and name in ("w_o", "w_gate", "w_up", "w_down"):
            return None
        declared.append(name)
        return nc.dram_tensor(name, list(shape), dt, kind="ExternalInput").ap()

    x_loc = din("x_loc", [SEQ, D])
    cT_d = din("cT", [128, 8])
    w_c = din("w_c", [D, 6 * D])
    bcT_d = din("b_cT", [128, 48])
    w_in = din("w_in", [D, INW])
    nbf_d = din("nbf", [8, 1])
    wa2_d = din("wa2", [17, 256])
    ggla_d = din("ggla", [128, 4])
    w_o = din("w_o", [D, D])
    lnp_d = din("lnp", [128, 32])
    w_gate = din("w_gate", [D, DFF])
    w_up = din("w_up", [D, DFF])
    w_down = din("w_down", [DFF, D])
    isb_d = din("isb", [128, 64])
    isbh_d = din("isbh", [128, 2, 64])
    nisb16_d = din("nisb16", [128, 64])
    maskb_d = din("maskb", [128, 64])
    isbrow_d = din("isbrow", [8, SEQ])
    cf_d = din("cf", [128, 5, 128])
    cb_d = din("cb", [128, 5, 128], BF16)
    rmask_d = din("rmask", [128, 512])
    out_d = nc.dram_tensor("out", [OWN, D], F32, kind="ExternalOutput").ap()

    dbg_outs = {"__inputs__": declared}
    skip = debug.get("skip", ())

    def dbg_out(name, shape, dt=F32):
        dbg_outs[name] = (list(shape), dt)
        return nc.dram_tensor("dbg_" + name, list(shape), dt, kind="ExternalOutput").ap()

    Kscr = nc.dram_tensor("Kscr", [512, SEQ], BF16).ap()
    Vscr = nc.dram_tensor("Vscr", [8, 128, 64, 65], BF16).ap()
    Qscr = nc.dram_tensor("Qscr", [8, 67, OWN], BF16).ap()
    NDscr = nc.dram_tensor("NDscr", [8, 3, SEQ], BF16).ap()
    bNDscr = Buf("NDscr")
    U2scr = nc.dram_tensor("U2scr", [128, 8, OWN], BF16).ap()
    X1scr = nc.dram_tensor("X1scr", [128, 8, OWN], F32).ap()
    bKscr, bVscr, bQscr, bU2scr, bX1scr = Buf("Kscr"), Buf("Vscr"), Buf("Qscr"), Buf("U2scr"), Buf("X1scr")
    bout = Buf("out")

    outer = ExitStack()
    P = Prog(nc, outer)

    def sb(stack, name, shape, dt):
        return Tile(stack.enter_context(nc.sbuf_tensor("t_" + name, list(shape), dt)), name)

    def ps(stack, name, shape=(128, 512), dt=F32):
        return Tile(stack.enter_context(nc.psum_tensor("p_" + name, list(shape), dt)), name, whole=True)

    def finish(early=False):
        P.finish_phase()

    cf = sb(outer, "cf", [128, 5, 128], F32)
    cb = sb(outer, "cb", [128, 5, 128], BF16)
    rmask = sb(outer, "rmask", [128, 512], F32)
    mv = sb(outer, "mv", [128, 96], F32)
    lnp = sb(outer, "lnp", [128, 32], F32)
    ggla = sb(outer, "ggla", [128, 4], F32)
    isb = sb(outer, "isb", [128, 64], F32)
    isbh = sb(outer, "isbh", [128, 2, 64], F32)
    nisb16 = sb(outer, "nisb16", [128, 64], F32)
    maskb = sb(outer, "maskb", [128, 64], F32)
    nbf = sb(outer, "nbf", [8, 1], F32)
    negDk = sb(outer, "negDk", [128, 64, 8], F32)
    ident_f = cf.t[:, 0, :]
    U_f = cf.t[:, 1, :]
    ones_f = cf.t[:, 2, :]
    E64_f = cf.t[:, 3, :]
    hm = cf.t[:, 4, 4:6]
    chunkind = cf.t[:, 4, 0:2]
    ident_b = cb.t[:, 0, :]
    trimask = cb.t[:, 1, :]
    M01x2 = cb.t[:, 2:4, :]
    shiftup = cb.t[:, 4, :]

    for (tl, src) in ((cf, cf_d), (cb, cb_d), (rmask, rmask_d), (lnp, lnp_d), (ggla, ggla_d), (isb, isb_d), (isbh, isbh_d),
                      (nisb16, nisb16_d), (maskb, maskb_d), (nbf, nbf_d)):
        P.dma("sync", tl.t[:], src, writes=[tl.b])

    def mvc(i, k):
        return mv.t[:, i * 8 + k:i * 8 + k + 1]

    attn_scope = ExitStack()
    attnT = sb(attn_scope, "attnT", [128, 8, OWN], BF16)

    if stop_after == "pA":
        P.op("gpsimd", lambda e: e.memset(attnT.t[:], 0.0), writes=[attnT.s(i) for i in range(8)])
        P.op("gpsimd", lambda e: e.memset(negDk.t[:], 0.0), writes=[negDk.b])

    POOL = "vector"
    SILU = AF.Identity if "nosilu" in debug.get("skip", ()) else AF.Silu

    def dump(name, tile_ap, shape, dt, reads):
        d = dbg_out(name, shape, dt)
        P.dma("sync", d, tile_ap, reads=reads, writes=[bout])

    win_scope = ExitStack()
    win = sb(win_scope, "win", [128, 8, INW], BF16)
    for k in range(8):
        P.dma("gpsimd", win.t[:, k, :], w_in[k * 128:(k + 1) * 128, :], writes=[win.s(k)], max_dma_last_dim=4096)

    with ExitStack() as ph:
        wcb = [sb(ph, "wcb%d" % i, [128, 8, 1024], F32) for i in range(2)]
        cT = sb(ph, "cT", [128, 8], F32)
        bcT = sb(ph, "bcT", [128, 48], F32)
        modT = sb(ph, "modT", [128, 48], F32)
        psmod = ps(ph, "psmod", [128, 48])
        P.dma("sync", cT.t[:], cT_d, writes=[cT.b])
        P.dma("sync", bcT.t[:], bcT_d, writes=[bcT.b])
        wc_v = w_c.rearrange("(k p) n -> p k n", p=128)
        for ch in range(6):
            buf = wcb[ch % 2]
            P.dma("sync" if ch % 2 == 0 else "scalar", buf.t[:], wc_v[:, :, ch * 1024:(ch + 1) * 1024], writes=[buf.b])
            for jc in range(8):
                j = ch * 8 + jc
                for k in range(8):
                    P.mm(psmod.t[:, j:j + 1], buf.t[:, k, jc * 128:(jc + 1) * 128], cT.t[:, k:k + 1],
                         start=(k == 0), stop=(k == 7), reads=[buf.b, cT.b], writes=[psmod.b],
                         inc=(k == 7 and jc == 7))
        P.op("vector", lambda e: e.tensor_tensor(modT.t[:], psmod.t[:], bcT.t[:], ALU.add),
             reads=[psmod.b, bcT.b], writes=[modT.b])
        m = lambda i: modT.t[:, i * 8:(i + 1) * 8]
        V = lambda i: mv.t[:, i * 8:(i + 1) * 8]
        g1, b1 = lnp.t[:, 0:8], lnp.t[:, 8:16]
        vops = [
            lambda e: e.tensor_scalar_add(V(0), m(1), 1.0),
            lambda e: e.tensor_copy(V(1), m(0)),
            lambda e: e.tensor_scalar_add(V(2), m(2), 1.0),
            lambda e: e.tensor_scalar_add(V(8), m(4), 1.0),
            lambda e: e.tensor_copy(V(9), m(3)),
            lambda e: e.tensor_scalar_add(V(7), m(5), 1.0),
            lambda e: e.tensor_tensor(V(3), g1, V(8), ALU.mult),
            lambda e: e.tensor_tensor(V(4), b1, V(8), ALU.mult),
            lambda e: e.tensor_tensor(V(4), V(4), V(9), ALU.add),
            lambda e: e.tensor_scalar_mul(V(5), g1, ALPHA),
            lambda e: e.tensor_scalar_mul(V(6), b1, ALPHA),
        ]
        for f in vops:
            P.op("vector", f, reads=[modT.b, lnp.b, mv.b], writes=[mv.b])
        if stop_after == "p0":
            dump("mv", mv.t[:], [128, 96], F32, [mv.b])
        finish()
    if stop_after == "p0":
        win_scope.close()
        attn_scope.close()
        outer.close()
        return nc, dbg_outs

    with ExitStack() as ph:
        wa2 = sb(ph, "wa2", [128, 256], BF16)
        xt = [sb(ph, "xt0", [128, 4, D], F32)] * 2
        uT = [sb(ph, "uT%d" % i, [128, 8, 512], BF16) for i in range(2)]
        kst = [sb(ph, "kst0", [128, 4, 512], BF16)] * 2
        vst = [sb(ph, "vst0", [128, 8, 4, 65], BF16)] * 2
        qst = sb(ph, "qst", [128, 4, 512], BF16)
        ffe = sb(ph, "ffe", [8, 512], F32)
        ffl = sb(ph, "ffl", [8, 512], F32)
        isbr = sb(ph, "isbr", [8, 512], F32)
        ones8 = sb(ph, "ones8", [8, 512], F32)
        Dt = [sb(ph, "Dt%d" % i, [128, 512], F32) for i in range(2)]
        dq = sb(ph, "dq", [8, 3, 512], BF16)
        dr = sb(ph, "dr", [8, 512], F32)
        nd = sb(ph, "nd", [8, 512], F32)
        nr = sb(ph, "nr", [8, 512], F32)
        nq = sb(ph, "nq", [8, 3, 512], BF16)
        gaa = sb(ph, "gaa", [128, 512], BF16)
        t1 = [sb(ph, "t1_%d" % i, [128, 256], F32) for i in range(2)]
        la = [sb(ph, "la%d" % i, [128, 256], F32) for i in range(2)]
        ee = [sb(ph, "ee%d" % i, [128, 256], F32) for i in range(2)]
        ke = [sb(ph, "ke%d" % i, [128, 2, 256], BF16) for i in range(2)]
        gvb = [sb(ph, "gvb%d" % i, [128, 512], BF16) for i in range(2)]
        dec = sb(ph, "dec", [128, 2, 8], F32)
        S = sb(ph, "S", [128, 2, 256], F32)
        Sb = sb(ph, "Sb", [128, 2, 256], BF16)
        lzT = sb(ph, "lzT", [128, 512], F32)
        bTp = sb(ph, "bTp", [128, 512], F32)
        ebn = sb(ph, "ebn", [128, 2, 512], F32)
        qd = [sb(ph, "qd%d" % i, [128, 512], BF16) for i in range(4)]
        ki = [sb(ph, "ki%d" % i, [128, 512], BF16) for i in range(2)]
        sr = sb(ph, "sr", [128, 4, 512], BF16)
        At = [sb(ph, "At%d" % i, [128, 2, 128], BF16) for i in range(2)]
        oT = sb(ph, "oT", [128, 4, 512], F32)
        sq = sb(ph, "sq", [128, 512], F32)
        rstd = sb(ph, "rstd", [128, 512], F32)
        go = sb(ph, "go", [128, 512], F32)
        ptr = [ps(ph, "ptr%d" % i) for i in range(2)]
        pj = [ps(ph, "pj%d" % i) for i in range(2)]
        pG0 = ps(ph, "pG0")
        pG1 = ps(ph, "pG1")
        pG2 = ps(ph, "pG2")
        pG3 = ps(ph, "pG3")
        pjn = [0]

        def next_pj():
            pjn[0] += 1
            return pj[pjn[0] % 2]

        P.op("gpsimd", lambda e: e.memset(wa2.t[:], 0.0), writes=[wa2.b])
        P.dma("gpsimd", wa2.t[0:17, :], wa2_d, writes=[wa2.b])
        P.op("gpsimd", lambda e: e.memset(gaa.t[:], 0.0), writes=[gaa.b])
        P.op("gpsimd", lambda e: e.memset(gaa.t[0:32, :], 1.0), writes=[gaa.b])
        P.op("gpsimd", lambda e: e.memset(gaa.t[32:64, :], 0.0), writes=[gaa.b])
        for i in range(2):
            P.op("gpsimd", lambda e, i=i: e.memset(Dt[i].t[:], 0.0), writes=[Dt[i].b])
        P.op("gpsimd", lambda e: e.memset(ones8.t[:], 1.0), writes=[ones8.b])
        P.op("gpsimd", lambda e: e.memset(S.t[:], 0.0), writes=[S.s(0), S.s(1)])
        P.op("gpsimd", lambda e: e.memset(vst[0].t[:], 1.0), writes=[vst[0].b])
        winb = [win.s(k) for k in range(8)]
        x_v = x_loc.rearrange("(T s p) f -> T p s f", p=128, s=4)
        Kscr_v = Kscr.rearrange("(m p) t -> p m t", p=128)
        Vscr_v = Vscr.rearrange("h p k c -> p h k c")
        Qscr_v = Qscr[:, 0:64, :].rearrange("(m two) d t -> two d m t", two=2)
        evq = [0]

        def evac_copy(out, in_, reads, writes):
            P.act(out, in_, AF.Copy, reads=reads, writes=writes)

        def proj_fm(col0, M, u, dst_fn):
            p = next_pj()
            for k in range(8):
                P.mm(p.t[:, :], win.t[:, k, col0:col0 + 128], u.t[:, k, :], start=(k == 0), stop=(k == 7),
                     reads=[winb[k], u.s(k)], writes=[p.b])
            dst_fn(p)

        def proj_tm(col0, N, u, s, dst_fn):
            p = next_pj()
            for k in range(8):
                P.mm(p.t[:, 0:N], u.t[:, k, s * 128:(s + 1) * 128], win.t[:, k, col0:col0 + N], start=(k == 0),
                     stop=(k == 7), reads=[winb[k], u.s(k)], writes=[p.b])
            dst_fn(p)

        tiles = debug.get("tiles", list(range(16)))
        first_own_done = [False]
        prevD = [None]
        pending_back = [None]
        for ti, T in enumerate(tiles):
            own = T >= 12
            to = T - 12
            X = xt[ti % 2]
            u = uT[ti % 2]
            P.dma("sync", X.t[:], x_v[T], writes=[X.b])
            for k in range(8):
                pt = ptr[k % 2]
                for s in range(4):
                    P.tr(pt.t[:, s * 128:(s + 1) * 128], X.t[:, s, k * 128:(k + 1) * 128], ident_f,
                         reads=[X.b, cf.b], writes=[pt.b], inc=(s == 3))
                P.act(u.t[:, k, :], pt.t[:], AF.Identity, reads=[pt.b, mv.b], writes=[u.s(k)],
                      bias=mvc(1, k), scale=mvc(0, k))
            KS = kst[ti % 2]
            VS = vst[ti % 2]

            def k_proj(m_):
                proj_fm(C_FK + m_ * 128, 128, u, lambda p: evac_copy(KS.t[:, m_, :], p.t[:], [p.b], [KS.b]))

            def v_proj(s_):
                proj_tm(C_FV, 512, u, s_,
                        lambda p: evac_copy(VS.t[:, :, s_, 0:64], p.t[:].rearrange("p (h d) -> p h d", d=64),
                                            [p.b], [VS.b]))
            P.dma("sync", isbr.t[:], isbrow_d[:, T * 512:(T + 1) * 512], writes=[isbr.b])
            Dc = Dt[ti % 2]

            def ff_post(p, Dc=Dc):
                P.act(ffe.t[:], p.t[0:8, :], AF.Exp, reads=[p.b, nbf.b], writes=[ffe.b], bias=nbf.t[:, 0:1], scale=-1.0)
                P.act(ffl.t[:], ffe.t[:], AF.Ln, reads=[ffe.b], writes=[ffl.b], bias=1.0)
                P.op("vector", lambda e: e.scalar_tensor_tensor(ffl.t[:], ffl.t[:], -1.0, isbr.t[:], ALU.mult, ALU.mult),
                     reads=[ffl.b, isbr.b], writes=[ffl.b])
                pd = prevD[0]
                init = 0.0 if pd is None else pd.t[0:8, 511:512]
                rds = [ones8.b, ffl.b] + ([] if pd is None else [pd.b])
                P.op("vector", lambda e, init=init: e.tensor_tensor_scan(Dc.t[0:8, :], ones8.t[:], ffl.t[:], init, ALU.mult,
                                                                         ALU.add), reads=rds, writes=[Dc.b])
                prevD[0] = Dc
                P.op("vector", lambda e: e.scalar_tensor_tensor(nd.t[:], isbr.t[:], -NEG, Dc.t[0:8, :], ALU.mult,
                                                                ALU.subtract), reads=[isbr.b, Dc.b], writes=[nd.b])
                P.op("vector", lambda e: e.tensor_scalar_add(nd.t[:], nd.t[:], NEG), reads=[nd.b], writes=[nd.b])
                P.op("vector", lambda e: e.tensor_copy(nq.t[:, 0, :], nd.t[:]), reads=[nd.b], writes=[nq.b])
                P.op("vector", lambda e: e.tensor_tensor(nr.t[:], nd.t[:], nq.t[:, 0, :], ALU.subtract),
                     reads=[nd.b, nq.b], writes=[nr.b])
                P.op("vector", lambda e: e.tensor_copy(nq.t[:, 1, :], nr.t[:]), reads=[nr.b], writes=[nq.b])
                P.op("vector", lambda e: e.tensor_tensor(nr.t[:], nr.t[:], nq.t[:, 1, :], ALU.subtract),
                     reads=[nr.b, nq.b], writes=[nr.b])
                P.op("vector", lambda e: e.tensor_copy(nq.t[:, 2, :], nr.t[:]), reads=[nr.b], writes=[nq.b])
                P.dma("sync", NDscr[:, :, T * 512:(T + 1) * 512], nq.t[:], reads=[nq.b], writes=[bNDscr])

            proj_fm(C_GA, 16, u, lambda p: evac_copy(gaa.t[0:16, :], p.t[0:16, :], [p.b], [gaa.b]))
            proj_fm(C_FF, 8, u, ff_post)
            if own and "dq" not in skip:
                P.op("vector", lambda e, Dc=Dc: e.tensor_copy(dq.t[:, 0, :], Dc.t[0:8, :]), reads=[Dc.b], writes=[dq.b])
                P.op("vector", lambda e, Dc=Dc: e.tensor_tensor(dr.t[:], Dc.t[0:8, :], dq.t[:, 0, :], ALU.subtract),
                     reads=[Dc.b, dq.b], writes=[dr.b])
                P.op("vector", lambda e: e.tensor_copy(dq.t[:, 1, :], dr.t[:]), reads=[dr.b], writes=[dq.b])
                P.op("vector", lambda e: e.tensor_tensor(dr.t[:], dr.t[:], dq.t[:, 1, :], ALU.subtract),
                     reads=[dr.b, dq.b], writes=[dr.b])
                P.op("vector", lambda e: e.tensor_copy(dq.t[:, 2, :], dr.t[:]), reads=[dr.b], writes=[dq.b])
                P.dma("sync", Qscr[:, 64:67, to * 512:(to + 1) * 512], dq.t[:], reads=[dq.b], writes=[bQscr])
            if own and "fq" not in skip:
                for m_ in range(4):
                    proj_fm(C_FQ + m_ * 128, 128, u,
                            lambda p, m_=m_: P.act(qst.t[:, m_, :], p.t[:], AF.Copy, reads=[p.b], writes=[qst.b],
                                                   scale=0.125))
                for two in range(2):
                    P.dma("sync", Qscr_v[two][:, :, to * 512:(to + 1) * 512], qst.t[two * 64:(two + 1) * 64, :, :],
                          reads=[qst.b], writes=[bQscr])

            gown = own and "gla" not in skip
            if gown:
                for pr in range(2):
                    pz = next_pj()
                    P.mm(pz.t[:], wa2.t[:, pr * 128:(pr + 1) * 128], gaa.t[:, :], True, True,
                         reads=[wa2.b, gaa.b], writes=[pz.b])
                    P.act(lzT.t[:], pz.t[:], AF.Exp, reads=[pz.b], writes=[lzT.b], scale=-1.0)
                    P.act(lzT.t[:], lzT.t[:], AF.Ln, reads=[lzT.b], writes=[lzT.b], bias=1.0)
                    P.op("vector", lambda e: e.tensor_tensor_scan(bTp.t[:], rmask.t[:], lzT.t[:], 0.0, ALU.mult, ALU.add),
                         reads=[rmask.b, lzT.b], writes=[bTp.b])
                    P.act(ebn.t[:, 0, :], bTp.t[:], AF.Exp, reads=[bTp.b], writes=[ebn.s(0)], scale=-1.0 / 16)
                    P.act(ebn.t[:, 1, :], bTp.t[:], AF.Exp, reads=[bTp.b], writes=[ebn.s(1)], scale=1.0 / 16)
                    def gq_post(p, pr=pr):
                        for hh in range(2):
                            P.op("vector", lambda e, hh=hh: e.scalar_tensor_tensor(
                                qd[2 * pr + hh].t[:], p.t[:], hm[:, hh:hh + 1], ebn.t[:, 0, :], ALU.mult, ALU.mult),
                                 reads=[p.b, ebn.s(0), cf.b], writes=[qd[2 * pr + hh].b])

                    proj_fm(C_GQ + pr * 128, 128, u, gq_post)
                    proj_fm(C_GK + pr * 128, 128, u, lambda p, pr=pr: P.op(
                        "vector", lambda e: e.tensor_tensor(ki[pr].t[:], p.t[:], ebn.t[:, 1, :], ALU.mult),
                        reads=[p.b, ebn.s(1)], writes=[ki[pr].b]))
                for h in range(4):
                    proj_fm(C_GR + h * 128, 128, u,
                            lambda p, h=h: P.act(sr.t[:, h, :], p.t[:], SILU, reads=[p.b], writes=[sr.s(h)]))
                if not first_own_done[0]:
                    first_own_done[0] = True
                    for pr in range(2):
                        P.op("vector", lambda e, pr=pr: e.tensor_copy(Sb.t[:, pr, :], S.t[:, pr, :]),
                             reads=[S.s(pr)], writes=[Sb.s(pr)])

            for s in range(4):
                kt = 4 * T + s
                i2 = s % 2
                P.mm(pG0.t[:, 0:256], gaa.t[:, s * 128:(s + 1) * 128], wa2.t[:, :], True, True,
                     reads=[gaa.b, wa2.b], writes=[pG0.s(0)])
                k_proj(s)
                P.act(t1[i2].t[:], pG0.t[:, 0:256], AF.Exp, reads=[pG0.s(0)], writes=[t1[i2].b], scale=-1.0)
                P.act(t1[i2].t[:], t1[i2].t[:], AF.Ln, reads=[t1[i2].b], writes=[t1[i2].b], bias=1.0)
                P.op("vector", lambda e, i2=i2, kt=kt: e.tensor_scalar(la[i2].t[:], t1[i2].t[:], nisb16.t[:, kt:kt + 1],
                                                                        None, ALU.mult),
                     reads=[t1[i2].b, nisb16.b], writes=[la[i2].b])
                P.mm(pG0.t[:, 256:512], U_f, la[i2].t[:], True, True, reads=[cf.b, la[i2].b], writes=[pG0.s(1)])
                v_proj(s)
                P.act(ee[i2].t[:], pG0.t[:, 256:512], AF.Exp, reads=[pG0.s(1)], writes=[ee[i2].b])
                def gk_post(p, i2=i2, kt=kt):
                    for half in range(2):
                        P.op("vector", lambda e, half=half: e.scalar_tensor_tensor(
                            ke[i2].t[:, half, :], p.t[:, 0:256], isbh.t[:, half, kt:kt + 1], ee[i2].t[:], ALU.mult,
                            ALU.mult), reads=[p.b, isbh.b, ee[i2].b], writes=[ke[i2].b])

                proj_tm(C_GK, 256, u, s, gk_post)
                proj_tm(C_GV, 512, u, s, lambda p, i2=i2: evac_copy(gvb[i2].t[:], p.t[:], [p.b], [gvb[i2].b]))
                for pr in range(2):
                    P.mm(pG2.t[:, pr * 8 + 2 * s:pr * 8 + 2 * s + 2], la[i2].t[:, pr * 128:(pr + 1) * 128], chunkind,
                         True, True, reads=[la[i2].b, cf.b], writes=[pG2.s("bl")])
                P.act(dec.t[:, :, 2 * s:2 * s + 2],
                      pG2.t[:, 0:16].rearrange("p (r c) -> p r c", c=8)[:, :, 2 * s:2 * s + 2], AF.Exp,
                      reads=[pG2.s("bl")], writes=[dec.b])
                def back(s=s, i2=i2, kt=kt, gown=gown):
                    if gown:
                        for pr in range(2):
                            for hh in range(2):
                                P.mm(pG2.t[:, 256 + hh * 128:256 + (hh + 1) * 128],
                                     ki[pr].t[:, s * 128:(s + 1) * 128],
                                     qd[2 * pr + hh].t[:, s * 128:(s + 1) * 128], True, True,
                                     reads=[ki[pr].b, qd[2 * pr + hh].b], writes=[pG2.s("A")], inc=(hh == 1))
                            P.op("vector", lambda e, pr=pr: e.tensor_tensor(
                                At[pr].t[:], pG2.t[:, 256:512].rearrange("p (h t) -> p h t", t=128), M01x2, ALU.mult),
                                 reads=[pG2.s("A"), cb.b], writes=[At[pr].b])
                    for half in range(2):
                        c = 2 * s + half
                        if gown:
                            for pr in range(2):
                                for hh in range(2):
                                    h = 2 * pr + hh
                                    oc = pG3.t[:, h * 128 + half * 64:h * 128 + half * 64 + 64]
                                    P.mm(oc, gvb[i2].t[:, h * 128:(h + 1) * 128], At[pr].t[:, hh, half * 64:half * 64 + 64],
                                         True, False, reads=[gvb[i2].b, At[pr].b], writes=[pG3.b], inc=False)
                                    P.mm(oc, Sb.t[:, pr, hh * 128:(hh + 1) * 128],
                                         qd[h].t[:, s * 128 + half * 64:s * 128 + half * 64 + 64],
                                         False, True, reads=[Sb.s(pr), qd[h].b], writes=[pG3.b], inc=True)
                        for pr in range(2):
                            P.mm(pG1.t[:, pr * 256:(pr + 1) * 256],
                                 ke[i2].t[:, half, pr * 128:(pr + 1) * 128],
                                 gvb[i2].t[:, pr * 256:(pr + 1) * 256], True, True,
                                 reads=[ke[i2].b, gvb[i2].b], writes=[pG1.s(pr)])
                            P.op("vector", lambda e, pr=pr, c=c: e.scalar_tensor_tensor(
                                S.t[:, pr, :], S.t[:, pr, :], dec.t[:, pr, c:c + 1], pG1.t[:, pr * 256:(pr + 1) * 256],
                                ALU.mult, ALU.add), reads=[S.s(pr), dec.b, pG1.s(pr)], writes=[S.s(pr)])
                            if gown:
                                P.op(POOL, lambda e, pr=pr: e.tensor_copy(Sb.t[:, pr, :], S.t[:, pr, :]),
                                     reads=[S.s(pr)], writes=[Sb.s(pr)])
                    if gown:
                        evac_copy(oT.t[:, :, s * 128:(s + 1) * 128], pG3.t[:].rearrange("p (h t) -> p h t", t=128),
                                  [pG3.b], [oT.b])

                if pending_back[0] is not None:
                    pending_back[0]()
                pending_back[0] = back
            if pending_back[0] is not None:
                pending_back[0]()
                pending_back[0] = None
            P.dma("sync", Kscr_v[:, :, T * 512:(T + 1) * 512], KS.t[:], reads=[KS.b], writes=[bKscr])
            P.dma("scalar", Vscr_v[:, :, 4 * T:4 * T + 4, :], VS.t[:], reads=[VS.b], writes=[bVscr])
            if gown and "gla_n" not in skip:
                for h in range(4):
                    P.op(POOL, lambda e, h=h: e.tensor_tensor(sq.t[:], oT.t[:, h, :], oT.t[:, h, :], ALU.mult),
                         reads=[oT.b], writes=[sq.b])
                    pm = next_pj()
                    P.mm(pm.t[:], ones_f, sq.t[:], True, True, reads=[cf.b, sq.b], writes=[pm.b])
                    if "gdump" in skip and h == 3:
                        dump("pm", pm.t[:], [128, 512], F32, [pm.b]) if False else None
                    if "fbias" in skip:
                        P.act(rstd.t[:], pm.t[:], AF.Ln, reads=[pm.b], writes=[rstd.b], bias=RMS_EPS, scale=1.0 / 128)
                    else:
                        P.act(rstd.t[:], pm.t[:], AF.Ln, reads=[pm.b, cf.b], writes=[rstd.b], bias=cf.t[:, 4, 2:3],
                              scale=1.0 / 128)
                    if "gdump" in skip and h == 3:
                        dump("lnv", rstd.t[:], [128, 512], F32, [rstd.b])
                    P.act(rstd.t[:], rstd.t[:], AF.Exp, reads=[rstd.b], writes=[rstd.b], scale=-0.5)
                    P.op("vector", lambda e, h=h: e.tensor_tensor(go.t[:], oT.t[:, h, :], rstd.t[:], ALU.mult),
                         reads=[oT.b, rstd.b], writes=[go.b])
                    P.op("vector", lambda e, h=h: e.scalar_tensor_tensor(
                        attnT.t[:, 4 + h, to * 512:(to + 1) * 512], go.t[:], ggla.t[:, h:h + 1], sr.t[:, h, :], ALU.mult,
                        ALU.mult), reads=[go.b, ggla.b, sr.s(h)], writes=[attnT.s(4 + h)])
        if stop_after == "pA" and "gdump" in skip:
            dump("oT", oT.t[:], [128, 4, 512], F32, [oT.b])
            dump("sr", sr.t[:], [128, 4, 512], BF16, [sr.s(h) for h in range(4)])
            dump("rstd", rstd.t[:], [128, 512], F32, [rstd.b])
            dump("go", go.t[:], [128, 512], F32, [go.b])
            dump("sq", sq.t[:], [128, 512], F32, [sq.b])
        if stop_after == "pA":
            dump("attnT", attnT.t[:], [128, 8, OWN], BF16, [attnT.s(4 + h) for h in range(4)])
            dump("negDk", negDk.t[:], [128, 64, 8], F32, [negDk.b])
            dump("S", S.t[:], [128, 2, 256], F32, [S.s(0), S.s(1)])
            nt = len(tiles)
            if nt == 16:
                dump("K", Kscr, [512, SEQ], BF16, [bKscr])
                dump("V", Vscr, [8, 128, 64, 65], BF16, [bVscr])
                dump("Q", Qscr, [8, 67, OWN], BF16, [bQscr])
            else:
                T0 = tiles[0]
                dump("K", Kscr[:, T0 * 512:(T0 + 1) * 512], [512, 512], BF16, [bKscr])
                dump("V", Vscr[:, :, 4 * T0:4 * T0 + 4, :], [8, 128, 4, 65], BF16, [bVscr])
                if tiles[-1] >= 12:
                    to_ = tiles[-1] - 12
                    dump("Q", Qscr[:, :, to_ * 512:(to_ + 1) * 512], [8, 67, 512], BF16, [bQscr])
        finish()
    win_scope.close()
    if stop_after == "pA":
        attn_scope.close()
        outer.close()
        return nc, dbg_outs
    wo_scope = ExitStack()
    if not early:
        wo = sb(wo_scope, "wo", [128, 8, D], BF16)
        for k in range(8):
            P.dma("gpsimd", wo.t[:, k, :], w_o[k * 128:(k + 1) * 128, :], writes=[wo.s(k)], max_dma_last_dim=4096)

    with ExitStack() as ph:
        kq = [sb(ph, "kq%d" % i, [70, SEQ], BF16) for i in range(2)]
        vq = [sb(ph, "vq%d" % i, [128, 64, 65], BF16) for i in range(2)]
        qq = [sb(ph, "qq%d" % i, [70, OWN], BF16) for i in range(2)]
        Pt = [sb(ph, "Pt%d" % i, [128, 512], BF16) for i in range(3)]
        Osb = sb(ph, "Osb", [65, 512], F32)
        R = sb(ph, "R", [128, 512], F32)
        lnr = sb(ph, "lnr", [65, 512], F32)
        On = sb(ph, "On", [128, 512], BF16)
        Sp = [ps(ph, "Sp%d" % i) for i in range(3)]
        Op = [ps(ph, "Op%d" % i) for i in range(2)]
        bcp = ps(ph, "bcp")
        shp = ps(ph, "shp")
        P.op("gpsimd", lambda e: e.memset(R.t[:], 0.0), writes=[R.b])
        P.op("gpsimd", lambda e: e.memset(On.t[:], 0.0), writes=[On.b])
        for i in range(2):
            P.op("gpsimd", lambda e, i=i: e.memset(kq[i].t[64:70, :], 1.0), writes=[kq[i].s("ones"), kq[i].b])
            P.op("gpsimd", lambda e, i=i: e.memset(qq[i].t[64:70, :], 1.0), writes=[qq[i].b])
        heads = debug.get("heads", list(range(8)))
        it = 0
        for hi, h in enumerate(heads):
            KQ, VQ, QQ = kq[hi % 2], vq[hi % 2], qq[hi % 2]
            P.dma("sync", KQ.t[0:64, :], Kscr[h * 64:(h + 1) * 64, :], reads=[bKscr], writes=[KQ.b])
            P.dma("scalar", VQ.t[:], Vscr[h], reads=[bVscr], writes=[VQ.b])
            P.dma("sync", QQ.t[0:67, :], Qscr[h], reads=[bQscr], writes=[QQ.b])
            P.dma("sync", KQ.t[67:70, :], NDscr[h], reads=[bNDscr], writes=[KQ.b])
            for qt in range(4):
                O = Op[qt % 2]
                order = [(0, 0, False)]
                for jj in range(4):
                    order.append((48 + 4 * qt + jj, jj * 128, True))
                for j in range(4 * qt):
                    order.append((48 + j, 0, False))
                for kt in range(1, 48):
                    order.append((kt, 0, False))
                n = len(order)

                def qk(i):
                    kt, c0, diag = order[i]
                    SP, PT = Sp[(it + i) % 3], Pt[(it + i) % 3]
                    P.mm(SP.t[:, c0:512], KQ.t[0:70, kt * 128:(kt + 1) * 128], QQ.t[0:70, qt * 512 + c0:(qt + 1) * 512],
                         True, not diag, reads=[KQ.b, KQ.s("ones"), QQ.b], writes=[SP.b], inc=(not diag))
                    if diag:
                        P.mm(SP.t[:, c0:c0 + 128], ident_b, trimask, False, True, reads=[cb.b], writes=[SP.b], inc=True)
                    P.act(PT.t[:, c0:512], SP.t[:, c0:512], AF.Exp, reads=[SP.b], writes=[PT.b])

                def pv(i):
                    kt, c0, diag = order[i]
                    PT = Pt[(it + i) % 3]
                    P.mm(O.t[0:65, c0:512], VQ.t[:, kt, 0:65], PT.t[:, c0:512], i == 0, i == n - 1,
                         reads=[VQ.b, PT.b], writes=[O.b], inc=(i == n - 1))

                LOOK = 2
                for i in range(n + LOOK):
                    if i < n:
                        qk(i)
                    if i >= LOOK:
                        pv(i - LOOK)
                it += n
                P.op("vector", lambda e, O=O: e.tensor_copy(Osb.t[:], O.t[0:65, :]), reads=[O.b], writes=[Osb.b])
                P.act(lnr.t[64:65, :], Osb.t[64:65, :], AF.Ln, reads=[Osb.b], writes=[lnr.b])
                P.act(R.t[64:65, :], lnr.t[64:65, :], AF.Exp, reads=[lnr.b], writes=[R.b], scale=-1.0)
                P.mm(bcp.t[:, :], E64_f, R.t[:], True, True, reads=[cf.b, R.b], writes=[bcp.b])
                cols = slice(qt * 512, (qt + 1) * 512)
                if h % 2 == 0:
                    P.op("vector", lambda e, h=h, cols=cols: e.tensor_tensor(attnT.t[0:64, h // 2, cols], Osb.t[0:64, :],
                                                                              bcp.t[0:64, :], ALU.mult),
                         reads=[Osb.b, bcp.b], writes=[attnT.s(h // 2)])
                else:
                    P.op("vector", lambda e: e.tensor_tensor(On.t[0:64, :], Osb.t[0:64, :], bcp.t[0:64, :], ALU.mult),
                         reads=[Osb.b, bcp.b], writes=[On.b])
                    P.mm(shp.t[:], shiftup, On.t[:], True, True, reads=[cb.b, On.b], writes=[shp.b])
                    P.op("vector", lambda e, h=h, cols=cols: e.tensor_copy(attnT.t[64:128, h // 2, cols],
                                                                            shp.t[64:128, :]),
                         reads=[shp.b], writes=[attnT.s(h // 2)])
        if stop_after == "pF":
            dump("attnT", attnT.t[:], [128, 8, OWN], BF16, [attnT.s(i) for i in range(8)])
        finish()
    if stop_after == "pF":
        wo_scope.close()
        attn_scope.close()
        outer.close()
        return nc, dbg_outs

    x_own = x_loc[NREST:SEQ, :].rearrange("(T s p) f -> T p s f", p=128, s=4)

    def layer_norm_stats(stk_tiles, z, n_out_fn):
        sqt, mean_sb, msq, var, pA, pB = stk_tiles
        for oc in range(8):
            P.mm(pA.t[:], ones_f, z.t[:, oc, :], oc == 0, oc == 7, reads=[cf.b, z.s(oc)], writes=[pA.b])
        for oc in range(8):
            P.op(POOL, lambda e, oc=oc: e.tensor_tensor(sqt.t[:, oc % 2, :], z.t[:, oc, :], z.t[:, oc, :], ALU.mult),
                 reads=[z.s(oc)], writes=[sqt.s(oc % 2)])
            P.mm(pB.t[:], ones_f, sqt.t[:, oc % 2, :], oc == 0, oc == 7, reads=[cf.b, sqt.s(oc % 2)], writes=[pB.b],
                 inc=True)
        P.op("vector", lambda e: e.tensor_scalar_mul(mean_sb.t[:], pA.t[:], 1.0 / D), reads=[pA.b], writes=[mean_sb.b])
        P.op("vector", lambda e: e.tensor_tensor(msq.t[:], mean_sb.t[:], mean_sb.t[:], ALU.mult), reads=[mean_sb.b],
             writes=[msq.b])
        P.op("vector", lambda e: e.scalar_tensor_tensor(var.t[:], pB.t[:], 1.0 / D, msq.t[:], ALU.mult, ALU.subtract),
             reads=[pB.b, msq.b], writes=[var.b])
        P.act(var.t[:], var.t[:], AF.Ln, reads=[var.b, cf.b], writes=[var.b], bias=cf.t[:, 4, 3:4])
        P.act(var.t[:], var.t[:], AF.Exp, reads=[var.b], writes=[var.b], scale=-0.5)
        for oc in range(8):
            P.op("vector", lambda e, oc=oc: e.tensor_tensor(z.t[:, oc, :], z.t[:, oc, :], mean_sb.t[:], ALU.subtract),
                 reads=[z.s(oc), mean_sb.b], writes=[z.s(oc)])
            P.op(POOL, lambda e, oc=oc: e.tensor_tensor(z.t[:, oc, :], z.t[:, oc, :], var.t[:], ALU.mult),
                 reads=[z.s(oc), var.b], writes=[z.s(oc)])
            n_out_fn(oc)

    with ExitStack() as ph:
        xt = [sb(ph, "xw%d" % i, [128, 4, D], F32) for i in range(2)]
        xz = [sb(ph, "xz%d" % i, [128, 8, 512], F32) for i in range(2)]
        sqt = sb(ph, "sqt", [128, 2, 512], F32)
        mean_sb = sb(ph, "mean", [128, 512], F32)
        msq = sb(ph, "msq", [128, 512], F32)
        var = sb(ph, "var", [128, 512], F32)
        u2s = [sb(ph, "u2s%d" % i, [128, 8, 512], BF16) for i in range(2)]
        x1s = [sb(ph, "x1s%d" % i, [128, 8, 512], F32) for i in range(2)]
        ptr = [ps(ph, "wtr%d" % i) for i in range(2)]
        pj = [ps(ph, "wpj%d" % i) for i in range(2)]
        pA = ps(ph, "wpA")
        pB = ps(ph, "wpB")

        def stageW_A(to):
            X = xt[to % 2]
            Z = xz[to % 2]
            cols = slice(to * 512, (to + 1) * 512)
            P.dma("sync", X.t[:], x_own[to], writes=[X.b])
            for k in range(8):
                pt = ptr[k % 2]
                for s in range(4):
                    P.tr(pt.t[:, s * 128:(s + 1) * 128], X.t[:, s, k * 128:(k + 1) * 128], ident_f, reads=[X.b, cf.b],
                         writes=[pt.b], inc=(s == 3))
                P.act(Z.t[:, k, :], pt.t[:], AF.Copy, reads=[pt.b], writes=[Z.s(k)], scale=ALPHA)
            for oc in range(8):
                p = pj[oc % 2]
                for kc in range(8):
                    P.mm(p.t[:], wo.t[:, kc, oc * 128:(oc + 1) * 128], attnT.t[:, kc, cols], kc == 0, kc == 7,
                         reads=[wo.s(kc), attnT.s(kc)], writes=[p.b])
                P.op("vector", lambda e, oc=oc, p=p: e.scalar_tensor_tensor(Z.t[:, oc, :], p.t[:], mvc(2, oc),
                                                                            Z.t[:, oc, :], ALU.mult, ALU.add),
                     reads=[p.b, mv.b, Z.s(oc)], writes=[Z.s(oc)])

        def stageW_B(to):
            Z = xz[to % 2]
            cols = slice(to * 512, (to + 1) * 512)
            U2, X1 = u2s[to % 2], x1s[to % 2]

            def n_out(oc):
                P.act(U2.t[:, oc, :], Z.t[:, oc, :], AF.Identity, reads=[Z.s(oc), mv.b], writes=[U2.b],
                      bias=mvc(4, oc), scale=mvc(3, oc))
                P.act(X1.t[:, oc, :], Z.t[:, oc, :], AF.Identity, reads=[Z.s(oc), mv.b], writes=[X1.b],
                      bias=mvc(6, oc), scale=mvc(5, oc))

            layer_norm_stats((sqt, mean_sb, msq, var, pA, pB), Z, n_out)
            P.dma("sync", U2scr[:, :, cols], U2.t[:], reads=[U2.b], writes=[bU2scr])
            P.dma("scalar", X1scr[:, :, cols], X1.t[:], reads=[X1.b], writes=[bX1scr])

        stageW_A(0)
        for to in range(4):
            if to + 1 < 4:
                stageW_A(to + 1)
            stageW_B(to)
        if stop_after == "pW":
            dump("U2", U2scr, [128, 8, OWN], BF16, [bU2scr])
            dump("X1", X1scr, [128, 8, OWN], F32, [bX1scr])
        finish()
    wo_scope.close()
    attn_scope.close()
    if stop_after == "pW":
        outer.close()
        return nc, dbg_outs

    with ExitStack() as hs:
        hT = sb(hs, "hT", [128, NFC, OWN], BF16)
        wd = sb(hs, "wd", [128, NFC, D], BF16)
        with ExitStack() as ph:
            u2 = sb(ph, "u2", [128, 8, OWN], BF16)
            wg = [sb(ph, "wg%d" % i, [128, 8, 256], BF16) for i in range(2)]
            wu = [sb(ph, "wu%d" % i, [128, 8, 256], BF16) for i in range(2)]
            sg = [sb(ph, "sg%d" % i, [128, 512], F32) for i in range(2)]
            pg = [ps(ph, "pg%d" % i) for i in range(2)]
            pu = [ps(ph, "pu%d" % i) for i in range(2)]
            P.dma("sync", u2.t[:], U2scr, reads=[bU2scr], writes=[u2.b])
            for fc in range(NFC):
                if fc == 1:
                    pass
            wd_pending = list(range(NFC))
            wg_v = w_gate.rearrange("(k p) n -> p k n", p=128)
            wu_v = w_up.rearrange("(k p) n -> p k n", p=128)
            cnt = 0
            for c2 in range(NFC // 2):
                WG, WU = wg[c2 % 2], wu[c2 % 2]
                P.dma("gpsimd", WG.t[:], wg_v[:, :, c2 * 256:(c2 + 1) * 256], writes=[WG.b])
                P.dma("gpsimd", WU.t[:], wu_v[:, :, c2 * 256:(c2 + 1) * 256], writes=[WU.b])
                for _ in range(2):
                    if wd_pending:
                        fc_ = wd_pending.pop(0)
                        P.dma("gpsimd", wd.t[:, fc_, :], w_down[fc_ * 128:(fc_ + 1) * 128, :], writes=[wd.s(fc_)],
                              max_dma_last_dim=4096)
                for f2 in range(2):
                    fc = c2 * 2 + f2
                    for to in range(4):
                        cols = slice(to * 512, (to + 1) * 512)
                        G, Uu, SG = pg[cnt % 2], pu[cnt % 2], sg[cnt % 2]
                        cnt += 1
                        for k in range(8):
                            P.mm(G.t[:], WG.t[:, k, f2 * 128:(f2 + 1) * 128], u2.t[:, k, cols], k == 0, k == 7,
                                 reads=[WG.b, u2.b], writes=[G.b])
                        for k in range(8):
                            P.mm(Uu.t[:], WU.t[:, k, f2 * 128:(f2 + 1) * 128], u2.t[:, k, cols], k == 0, k == 7,
                                 reads=[WU.b, u2.b], writes=[Uu.b])
                        P.act(SG.t[:], G.t[:], SILU, reads=[G.b], writes=[SG.b])
                        P.op("vector", lambda e, fc=fc, cols=cols, SG=SG, Uu=Uu: e.tensor_tensor(
                            hT.t[:, fc, cols], SG.t[:], Uu.t[:], ALU.mult), reads=[SG.b, Uu.b], writes=[hT.s(fc)])
            finish()
        with ExitStack() as ph:
            x1z = [sb(ph, "x1z%d" % i, [128, 8, 512], F32) for i in range(2)]
            sqt = sb(ph, "sqt2", [128, 2, 512], F32)
            mean_sb = sb(ph, "mean2", [128, 512], F32)
            msq = sb(ph, "msq2", [128, 512], F32)
            var = sb(ph, "var2", [128, 512], F32)
            ot = [sb(ph, "ot%d" % i, [128, D], F32) for i in range(2)]
            pj = [ps(ph, "fpj%d" % i) for i in range(2)]
            pA = ps(ph, "fpA")
            pB = ps(ph, "fpB")
            ptr = [ps(ph, "ftr%d" % i) for i in range(2)]

            def stage2_A(to):
                cols = slice(to * 512, (to + 1) * 512)
                Z = x1z[to % 2]
                P.dma("sync", Z.t[:], X1scr[:, :, cols], reads=[bX1scr], writes=[Z.s(oc) for oc in range(8)])
                for oc in range(8):
                    p = pj[oc % 2]
                    for fc in range(NFC):
                        P.mm(p.t[:], wd.t[:, fc, oc * 128:(oc + 1) * 128], hT.t[:, fc, cols], fc == 0, fc == NFC - 1,
                             reads=[wd.s(fc), hT.s(fc)], writes=[p.b])
                    P.op("vector", lambda e, oc=oc, p=p: e.scalar_tensor_tensor(
                        Z.t[:, oc, :], p.t[:], mvc(7, oc), Z.t[:, oc, :], ALU.mult, ALU.add),
                         reads=[p.b, mv.b, Z.s(oc)], writes=[Z.s(oc)])

            def stage2_B(to):
                Z = x1z[to % 2]

                def n_out2(oc):
                    P.act(Z.t[:, oc, :], Z.t[:, oc, :], AF.Identity, reads=[Z.s(oc), lnp.b], writes=[Z.s(oc)],
                          bias=lnp.t[:, 24 + oc:25 + oc], scale=lnp.t[:, 16 + oc:17 + oc])

                layer_norm_stats((sqt, mean_sb, msq, var, pA, pB), Z, n_out2)
                for s in range(4):
                    OT = ot[s % 2]
                    for half in range(2):
                        pt = ptr[half]
                        for q4 in range(4):
                            oc = half * 4 + q4
                            P.tr(pt.t[:, q4 * 128:(q4 + 1) * 128], Z.t[:, oc, s * 128:(s + 1) * 128], ident_f,
                                 reads=[Z.s(oc), cf.b], writes=[pt.b], inc=(q4 == 3))
                        if half == 0:
                            P.op("vector", lambda e, OT=OT, pt=pt: e.tensor_copy(OT.t[:, 0:512], pt.t[:]),
                                 reads=[pt.b], writes=[OT.s(0)])
                        else:
                            P.act(OT.t[:, 512:1024], pt.t[:], AF.Copy, reads=[pt.b], writes=[OT.s(1)])
                    r0 = to * 512 + s * 128
                    P.dma("sync", out_d[r0:r0 + 128, :], OT.t[:], reads=[OT.s(0), OT.s(1)], writes=[bout])

            stage2_A(0)
            for to in range(4):
                if to + 1 < 4:
                    stage2_A(to + 1)
                stage2_B(to)
            finish()
    outer.close()
    return nc, dbg_outs


def _consts():
    idx = np.arange(128)
    s_, t_ = idx[:, None], idx[None, :]
    same = (s_ // 64) == (t_ // 64)
    cf = np.zeros((128, 5, 128), np.float32)
    cf[:, 0, :] = np.eye(128, dtype=np.float32)
    cf[:, 1, :] = ((s_ > t_) & same).astype(np.float32)
    cf[:, 2, :] = 1.0
    cf[64, 3, 0:64] = 1.0
    cf[:, 4, 0] = (idx < 64)
    cf[:, 4, 1] = (idx >= 64)
    cf[:, 4, 2] = RMS_EPS
    cf[:, 4, 3] = LN_EPS
    cf[:, 4, 4] = np.where(idx < 64, 0.125, 0.0)
    cf[:, 4, 5] = np.where(idx >= 64, 0.125, 0.0)
    cb = np.zeros((128, 5, 128), np.float32)
    cb[:, 0, :] = np.eye(128, dtype=np.float32)
    cb[:, 1, :] = np.where(s_ > t_, NEG, 0.0)
    cb[:, 2, :] = ((s_ <= t_) & same).astype(np.float32)
    cb[:, 3, :] = cb[:, 2, :]
    cb[0:64, 4, 64:128] = np.eye(64, dtype=np.float32)
    rmask = np.ones((128, 512), np.float32)
    rmask[:, ::64] = 0.0
    return cf, cb.astype(ml_dtypes.bfloat16), rmask


def make_in_maps(x, c, w_c, b_c, w_in, b_f, w_a2, b_a, g_gla, w_o, ln1_g, ln1_b, w_gate, w_up, w_down, ln2_g, ln2_b):
    f = lambda a: np.ascontiguousarray(np.asarray(a, dtype=np.float32))
    x, c, w_c, b_c, w_in = f(x), f(c), f(w_c), f(b_c), f(w_in)
    cf, cb, rmask = _consts()
    pp = lambda v: np.ascontiguousarray(f(v).reshape(-1, 128).T)
    shared = {
        "w_c": w_c, "b_cT": pp(b_c), "w_in": w_in, "nbf": np.ascontiguousarray(-f(b_f).reshape(8, 1)),
        "wa2": np.ascontiguousarray(np.concatenate([f(w_a2), f(b_a)[None, :]], axis=0)),
        "ggla": pp(g_gla), "w_o": f(w_o),
        "lnp": np.ascontiguousarray(np.concatenate([pp(ln1_g), pp(ln1_b), pp(ln2_g), pp(ln2_b)], axis=1)),
        "w_gate": f(w_gate), "w_up": f(w_up), "w_down": f(w_down), "cf": cf, "cb": cb, "rmask": rmask,
    }
    in_maps = []
    for core in range(8):
        b, j = core // 4, core % 4
        own = np.arange(OWN * j, OWN * (j + 1))
        rest = np.concatenate([np.arange(0, OWN * j), np.arange(OWN * (j + 1), SEQ)])
        loc = np.concatenate([rest, own])
        isb_loc = np.concatenate([(rest < OWN * j), np.ones(OWN, bool)]).astype(np.float32)
        m = dict(shared)
        m["x_loc"] = np.ascontiguousarray(x[b][loc])
        m["cT"] = pp(c[b])
        m["isb"] = np.ascontiguousarray(isb_loc.reshape(64, 128).T)
        ih = np.zeros((128, 2, 64), np.float32)
        ih[0:64, 0, :] = m["isb"][0:64, :]
        ih[64:128, 1, :] = m["isb"][64:128, :]
        m["isbh"] = ih
        m["nisb16"] = np.ascontiguousarray((-isb_loc / 16.0).astype(np.float32).reshape(64, 128).T)
        m["maskb"] = np.ascontiguousarray(((1.0 - isb_loc) * NEG).astype(np.float32).reshape(64, 128).T)
        m["isbrow"] = np.ascontiguousarray(np.broadcast_to(isb_loc[None, :], (8, SEQ)))
        in_maps.append(m)
    return in_maps


def kernel(**inputs):
    nc, dbgo = build(_DEBUG)
    in_maps = make_in_maps(**inputs)
    names = set(dbgo["__inputs__"])
    in_maps = [{k: v for k, v in m.items() if k in names} for m in in_maps]
    res = run_bass_kernel_spmd(nc, in_maps, core_ids=list(range(8)))
    if _DEBUG:
        return res
    out = np.empty((2, SEQ, D), np.float32)
    for core in range(8):
        b, j = core // 4, core % 4
        out[b, OWN * j:OWN * (j + 1), :] = res.results[core]["out"]
    return out
```

```python
import numpy as np
import ml_dtypes
from contextlib import ExitStack

import concourse.bass as bass
import concourse.mybir as mybir
from concourse.bass_utils import run_bass_kernel_spmd

F32 = mybir.dt.float32
BF16 = mybir.dt.bfloat16
AF = mybir.ActivationFunctionType
ALU = mybir.AluOpType

D = 1024
SEQ = 8192
OWN = 2048
NREST = SEQ - OWN
DFF = 2816
NFC = DFF // 128
ALPHA = 2.0 ** 0.25
LN_EPS = 1e-5
RMS_EPS = 1e-6
NEG = -30000.0
C_FQ, C_FK, C_FV, C_FF, C_GQ, C_GK, C_GV, C_GA, C_GR = 0, 512, 1024, 1536, 1544, 1800, 2056, 2568, 2584
INW = 3096

_DEBUG = None


class Buf:
    __slots__ = ("name", "lw", "rd", "dsem", "dcnt")

    def __init__(self, name):
        self.name = name
        self.lw = None
        self.rd = {}
        self.dsem = None
        self.dcnt = 0


class Tile:
    def __init__(self, t, name, whole=False):
        self.t = t
        self.name = name
        self.b = Buf(name)
        self._subs = {}
        self.whole = whole

    def s(self, key):
        if self.whole:
            return self.b
        if key not in self._subs:
            self._subs[key] = Buf("%s.%s" % (self.name, key))
        return self._subs[key]


ENGS = ("sync", "scalar", "vector", "gpsimd", "tensor")


class _Rec:
    def __init__(self):
        self.call = None

    def __getattr__(self, name):
        def f(*a, **kw):
            self.call = (name, a, kw)
            return self
        return f


class Prog:
    def __init__(self, nc, stack):
        self.nc = nc
        self.stack = stack
        self.q = {e: [] for e in ENGS}
        self.sem = {e: stack.enter_context(nc.semaphore("s_" + e)) for e in ENGS}
        self.cnt = {e: 0 for e in ENGS}
        self.seen = {e: {} for e in ENGS}
        self.dma_bufs = []
        self.nblock = 0
        self.rr = 0

    def _tokcount(self, tok):
        kind, owner, c = tok
        if kind == "d":
            return owner.dcnt
        return c

    def _deps(self, reads, writes):
        deps = {}

        def add(tok):
            if tok is None:
                return
            kind, owner, c = tok
            key = ("d", owner.name) if kind == "d" else ("e", owner)
            c = self._tokcount(tok)
            if key not in deps or deps[key][1] < c:
                deps[key] = (tok, c)

        for b in reads:
            add(b.lw)
        for b in writes:
            add(b.lw)
            for tok in b.rd.values():
                add(tok)
        return deps

    def _emit_waits(self, eng, deps):
        for key, (tok, c) in deps.items():
            kind, owner, _ = tok
            if kind == "e" and owner == eng and eng == "tensor":
                continue
            if self.seen[eng].get(key, 0) >= c:
                continue
            self.seen[eng][key] = c
            h = owner.dsem if kind == "d" else self.sem[owner]
            self.q[eng].append(("wait", h, c))

    def op(self, eng, fn, reads=(), writes=(), inc=True):
        deps = self._deps(reads, writes)
        self._emit_waits(eng, deps)
        if inc:
            self.cnt[eng] += 1
            c = self.cnt[eng]
        else:
            assert eng == "tensor"
            c = self.cnt[eng] + 1
        rec = _Rec()
        fn(rec)
        assert rec.call is not None
        self.q[eng].append(("op", rec.call, inc))
        tok = ("e", eng, c)
        for b in writes:
            b.lw = tok
            b.rd = {}
        for b in reads:
            b.rd[("e", eng)] = tok

    def dma(self, q, out, in_, reads=(), writes=(), **kw):
        deps = self._deps(reads, writes)
        self._emit_waits(q, deps)
        b = writes[0]
        if b.dsem is None:
            b.dsem = self.stack.enter_context(self.nc.semaphore("d%d" % len(self.dma_bufs)))
            self.dma_bufs.append(b)
        b.dcnt += 16
        tok = ("d", b, b.dcnt)
        self.q[q].append(("dma", out, in_, b.dsem, kw))
        for w in writes:
            w.lw = tok
            w.rd = {}
        for r in reads:
            r.rd[("d", b.name)] = tok

    def mm(self, out, lhsT, rhs, start, stop, reads, writes, inc=None):
        if inc is None:
            inc = stop
        self.op("tensor", lambda e, o=out, l=lhsT, r=rhs, a=start, z=stop: e.matmul(o, l, r, start=a, stop=z),
                reads=reads, writes=writes, inc=inc)

    def tr(self, out, in_, ident, reads, writes, inc=True):
        self.op("tensor", lambda e, o=out, i=in_, d=ident: e.transpose(o, i, d), reads=reads, writes=writes, inc=inc)

    def act(self, out, in_, func, reads, writes, bias=None, scale=None):
        kw = {}
        if bias is not None:
            kw["bias"] = bias
        if scale is not None:
            kw["scale"] = scale
        self.op("scalar", lambda e, o=out, i=in_, f=func, k=kw: e.activation(o, i, f, **k), reads=reads, writes=writes)

    def finish_phase(self):
        nc = self.nc
        for b in self.dma_bufs:
            key = ("d", b.name)
            if self.seen["sync"].get(key, 0) < b.dcnt:
                self.seen["sync"][key] = b.dcnt
                self.q["sync"].append(("wait", b.dsem, b.dcnt))
        for e in ENGS:
            if e == "sync":
                continue
            key = ("e", e)
            if self.cnt[e] > 0 and self.seen["sync"].get(key, 0) < self.cnt[e]:
                self.seen["sync"][key] = self.cnt[e]
                self.q["sync"].append(("wait", self.sem[e], self.cnt[e]))
        q = self.q
        sem = self.sem

        def replay(e, name):
            for it in q[name]:
                if it[0] == "wait":
                    e.wait_ge(it[1], it[2])
                elif it[0] == "op":
                    nm, a, kw_ = it[1]
                    ins = getattr(e, nm)(*a, **kw_)
                    if it[2]:
                        ins.then_inc(sem[name], 1)
                else:
                    _, out, in_, dsem, kw = it
                    e.dma_start(out=out, in_=in_, **kw).then_inc(dsem, 16)

        with nc.Block() as block:
            @block.sync
            def _(e):
                replay(e, "sync")

            @block.scalar
            def _(e):
                replay(e, "scalar")

            @block.vector
            def _(e):
                replay(e, "vector")

            @block.gpsimd
            def _(e):
                replay(e, "gpsimd")

            @block.tensor
            def _(e):
                replay(e, "tensor")
        self.q = {e: [] for e in ENGS}
        self.nblock += 1


def build(debug=None):
    debug = debug or {}
    stop_after = debug.get("stop")
    nc = bass.Bass("TRN2", target_bir_lowering=False)

    declared = []
    early = stop_after in ("p0", "pA", "pF")

    def din(name, shape, dt=F32):
        if early and name in ("w_o", "w_gate", "w_up", "w_down"):
            return None
        declared.append(name)
        return nc.dram_tensor(name, list(shape), dt, kind="ExternalInput").ap()

    x_loc = din("x_loc", [SEQ, D])
    cT_d = din("cT", [128, 8])
    w_c = din("w_c", [D, 6 * D])
    bcT_d = din("b_cT", [128, 48])
    w_in = din("w_in", [D, INW])
    nbf_d = din("nbf", [8, 1])
    wa2_d = din("wa2", [17, 256])
    ggla_d = din("ggla", [128, 4])
    w_o = din("w_o", [D, D])
    lnp_d = din("lnp", [128, 32])
    w_gate = din("w_gate", [D, DFF])
    w_up = din("w_up", [D, DFF])
    w_down = din("w_down", [DFF, D])
    isb_d = din("isb", [128, 64])
    isbh_d = din("isbh", [128, 2, 64])
    nisb16_d = din("nisb16", [128, 64])
    maskb_d = din("maskb", [128, 64])
    isbrow_d = din("isbrow", [8, SEQ])
    cf_d = din("cf", [128, 5, 128])
    cb_d = din("cb", [128, 5, 128], BF16)
    rmask_d = din("rmask", [128, 512])
    out_d = nc.dram_tensor("out", [OWN, D], F32, kind="ExternalOutput").ap()

    dbg_outs = {"__inputs__": declared}
    skip = debug.get("skip", ())

    def dbg_out(name, shape, dt=F32):
        dbg_outs[name] = (list(shape), dt)
        return nc.dram_tensor("dbg_" + name, list(shape), dt, kind="ExternalOutput").ap()

    Kscr = nc.dram_tensor("Kscr", [512, SEQ], BF16).ap()
    Vscr = nc.dram_tensor("Vscr", [8, 128, 64, 65], BF16).ap()
    Qscr = nc.dram_tensor("Qscr", [8, 67, OWN], BF16).ap()
    NDscr = nc.dram_tensor("NDscr", [8, 3, SEQ], BF16).ap()
    bNDscr = Buf("NDscr")
    U2scr = nc.dram_tensor("U2scr", [128, 8, OWN], BF16).ap()
    X1scr = nc.dram_tensor("X1scr", [128, 8, OWN], F32).ap()
    bKscr, bVscr, bQscr, bU2scr, bX1scr = Buf("Kscr"), Buf("Vscr"), Buf("Qscr"), Buf("U2scr"), Buf("X1scr")
    bout = Buf("out")

    outer = ExitStack()
    P = Prog(nc, outer)

    def sb(stack, name, shape, dt):
        return Tile(stack.enter_context(nc.sbuf_tensor("t_" + name, list(shape), dt)), name)

    def ps(stack, name, shape=(128, 512), dt=F32):
        return Tile(stack.enter_context(nc.psum_tensor("p_" + name, list(shape), dt)), name, whole=True)

    def finish(early=False):
        P.finish_phase()

    cf = sb(outer, "cf", [128, 5, 128], F32)
    cb = sb(outer, "cb", [128, 5, 128], BF16)
    rmask = sb(outer, "rmask", [128, 512], F32)
    mv = sb(outer, "mv", [128, 96], F32)
    lnp = sb(outer, "lnp", [128, 32], F32)
    ggla = sb(outer, "ggla", [128, 4], F32)
    isb = sb(outer, "isb", [128, 64], F32)
    isbh = sb(outer, "isbh", [128, 2, 64], F32)
    nisb16 = sb(outer, "nisb16", [128, 64], F32)
    maskb = sb(outer, "maskb", [128, 64], F32)
    nbf = sb(outer, "nbf", [8, 1], F32)
    negDk = sb(outer, "negDk", [128, 64, 8], F32)
    ident_f = cf.t[:, 0, :]
    U_f = cf.t[:, 1, :]
    ones_f = cf.t[:, 2, :]
    E64_f = cf.t[:, 3, :]
    hm = cf.t[:, 4, 4:6]
    chunkind = cf.t[:, 4, 0:2]
    ident_b = cb.t[:, 0, :]
    trimask = cb.t[:, 1, :]
    M01x2 = cb.t[:, 2:4, :]
    shiftup = cb.t[:, 4, :]

    for (tl, src) in ((cf, cf_d), (cb, cb_d), (rmask, rmask_d), (lnp, lnp_d), (ggla, ggla_d), (isb, isb_d), (isbh, isbh_d),
                      (nisb16, nisb16_d), (maskb, maskb_d), (nbf, nbf_d)):
        P.dma("sync", tl.t[:], src, writes=[tl.b])

    def mvc(i, k):
        return mv.t[:, i * 8 + k:i * 8 + k + 1]

    attn_scope = ExitStack()
    attnT = sb(attn_scope, "attnT", [128, 8, OWN], BF16)

    if stop_after == "pA":
        P.op("gpsimd", lambda e: e.memset(attnT.t[:], 0.0), writes=[attnT.s(i) for i in range(8)])
        P.op("gpsimd", lambda e: e.memset(negDk.t[:], 0.0), writes=[negDk.b])

    POOL = "vector"
    SILU = AF.Identity if "nosilu" in debug.get("skip", ()) else AF.Silu

    def dump(name, tile_ap, shape, dt, reads):
        d = dbg_out(name, shape, dt)
        P.dma("sync", d, tile_ap, reads=reads, writes=[bout])

    win_scope = ExitStack()
    win = sb(win_scope, "win", [128, 8, INW], BF16)
    for k in range(8):
        P.dma("gpsimd", win.t[:, k, :], w_in[k * 128:(k + 1) * 128, :], writes=[win.s(k)], max_dma_last_dim=4096)

    with ExitStack() as ph:
        wcb = [sb(ph, "wcb%d" % i, [128, 8, 1024], F32) for i in range(2)]
        cT = sb(ph, "cT", [128, 8], F32)
        bcT = sb(ph, "bcT", [128, 48], F32)
        modT = sb(ph, "modT", [128, 48], F32)
        psmod = ps(ph, "psmod", [128, 48])
        P.dma("sync", cT.t[:], cT_d, writes=[cT.b])
        P.dma("sync", bcT.t[:], bcT_d, writes=[bcT.b])
        wc_v = w_c.rearrange("(k p) n -> p k n", p=128)
        for ch in range(6):
            buf = wcb[ch % 2]
            P.dma("sync" if ch % 2 == 0 else "scalar", buf.t[:], wc_v[:, :, ch * 1024:(ch + 1) * 1024], writes=[buf.b])
            for jc in range(8):
                j = ch * 8 + jc
                for k in range(8):
                    P.mm(psmod.t[:, j:j + 1], buf.t[:, k, jc * 128:(jc + 1) * 128], cT.t[:, k:k + 1],
                         start=(k == 0), stop=(k == 7), reads=[buf.b, cT.b], writes=[psmod.b],
                         inc=(k == 7 and jc == 7))
        P.op("vector", lambda e: e.tensor_tensor(modT.t[:], psmod.t[:], bcT.t[:], ALU.add),
             reads=[psmod.b, bcT.b], writes=[modT.b])
        m = lambda i: modT.t[:, i * 8:(i + 1) * 8]
        V = lambda i: mv.t[:, i * 8:(i + 1) * 8]
        g1, b1 = lnp.t[:, 0:8], lnp.t[:, 8:16]
        vops = [
            lambda e: e.tensor_scalar_add(V(0), m(1), 1.0),
            lambda e: e.tensor_copy(V(1), m(0)),
            lambda e: e.tensor_scalar_add(V(2), m(2), 1.0),
            lambda e: e.tensor_scalar_add(V(8), m(4), 1.0),
            lambda e: e.tensor_copy(V(9), m(3)),
            lambda e: e.tensor_scalar_add(V(7), m(5), 1.0),
            lambda e: e.tensor_tensor(V(3), g1, V(8), ALU.mult),
            lambda e: e.tensor_tensor(V(4), b1, V(8), ALU.mult),
            lambda e: e.tensor_tensor(V(4), V(4), V(9), ALU.add),
            lambda e: e.tensor_scalar_mul(V(5), g1, ALPHA),
            lambda e: e.tensor_scalar_mul(V(6), b1, ALPHA),
        ]
        for f in vops:
            P.op("vector", f, reads=[modT.b, lnp.b, mv.b], writes=[mv.b])
        if stop_after == "p0":
            dump("mv", mv.t[:], [128, 96], F32, [mv.b])
        finish()
    if stop_after == "p0":
        win_scope.close()
        attn_scope.close()
        outer.close()
        return nc, dbg_outs

    with ExitStack() as ph:
        wa2 = sb(ph, "wa2", [128, 256], BF16)
        xt = [sb(ph, "xt0", [128, 4, D], F32)] * 2
        uT = [sb(ph, "uT%d" % i, [128, 8, 512], BF16) for i in range(2)]
        kst = [sb(ph, "kst0", [128, 4, 512], BF16)] * 2
        vst = [sb(ph, "vst0", [128, 8, 4, 65], BF16)] * 2
        qst = sb(ph, "qst", [128, 4, 512], BF16)
        ffe = sb(ph, "ffe", [8, 512], F32)
        ffl = sb(ph, "ffl", [8, 512], F32)
        isbr = sb(ph, "isbr", [8, 512], F32)
        ones8 = sb(ph, "ones8", [8, 512], F32)
        Dt = [sb(ph, "Dt%d" % i, [128, 512], F32) for i in range(2)]
        dq = sb(ph, "dq", [8, 3, 512], BF16)
        dr = sb(ph, "dr", [8, 512], F32)
        nd = sb(ph, "nd", [8, 512], F32)
        nr = sb(ph, "nr", [8, 512], F32)
        nq = sb(ph, "nq", [8, 3, 512], BF16)
        gaa = sb(ph, "gaa", [128, 512], BF16)
        t1 = [sb(ph, "t1_%d" % i, [128, 256], F32) for i in range(2)]
        la = [sb(ph, "la%d" % i, [128, 256], F32) for i in range(2)]
        ee = [sb(ph, "ee%d" % i, [128, 256], F32) for i in range(2)]
        ke = [sb(ph, "ke%d" % i, [128, 2, 256], BF16) for i in range(2)]
        gvb = [sb(ph, "gvb%d" % i, [128, 512], BF16) for i in range(2)]
        dec = sb(ph, "dec", [128, 2, 8], F32)
        S = sb(ph, "S", [128, 2, 256], F32)
        Sb = sb(ph, "Sb", [128, 2, 256], BF16)
        lzT = sb(ph, "lzT", [128, 512], F32)
        bTp = sb(ph, "bTp", [128, 512], F32)
        ebn = sb(ph, "ebn", [128, 2, 512], F32)
        qd = [sb(ph, "qd%d" % i, [128, 512], BF16) for i in range(4)]
        ki = [sb(ph, "ki%d" % i, [128, 512], BF16) for i in range(2)]
        sr = sb(ph, "sr", [128, 4, 512], BF16)
        At = [sb(ph, "At%d" % i, [128, 2, 128], BF16) for i in range(2)]
        oT = sb(ph, "oT", [128, 4, 512], F32)
        sq = sb(ph, "sq", [128, 512], F32)
        rstd = sb(ph, "rstd", [128, 512], F32)
        go = sb(ph, "go", [128, 512], F32)
        ptr = [ps(ph, "ptr%d" % i) for i in range(2)]
        pj = [ps(ph, "pj%d" % i) for i in range(2)]
        pG0 = ps(ph, "pG0")
        pG1 = ps(ph, "pG1")
        pG2 = ps(ph, "pG2")
        pG3 = ps(ph, "pG3")
        pjn = [0]

        def next_pj():
            pjn[0] += 1
            return pj[pjn[0] % 2]

        P.op("gpsimd", lambda e: e.memset(wa2.t[:], 0.0), writes=[wa2.b])
        P.dma("gpsimd", wa2.t[0:17, :], wa2_d, writes=[wa2.b])
        P.op("gpsimd", lambda e: e.memset(gaa.t[:], 0.0), writes=[gaa.b])
        P.op("gpsimd", lambda e: e.memset(gaa.t[0:32, :], 1.0), writes=[gaa.b])
        P.op("gpsimd", lambda e: e.memset(gaa.t[32:64, :], 0.0), writes=[gaa.b])
        for i in range(2):
            P.op("gpsimd", lambda e, i=i: e.memset(Dt[i].t[:], 0.0), writes=[Dt[i].b])
        P.op("gpsimd", lambda e: e.memset(ones8.t[:], 1.0), writes=[ones8.b])
        P.op("gpsimd", lambda e: e.memset(S.t[:], 0.0), writes=[S.s(0), S.s(1)])
        P.op("gpsimd", lambda e: e.memset(vst[0].t[:], 1.0), writes=[vst[0].b])
        winb = [win.s(k) for k in range(8)]
        x_v = x_loc.rearrange("(T s p) f -> T p s f", p=128, s=4)
        Kscr_v = Kscr.rearrange("(m p) t -> p m t", p=128)
        Vscr_v = Vscr.rearrange("h p k c -> p h k c")
        Qscr_v = Qscr[:, 0:64, :].rearrange("(m two) d t -> two d m t", two=2)
        evq = [0]

        def evac_copy(out, in_, reads, writes):
            P.op("vector", lambda e, o=out, i=in_: e.tensor_copy(o, i), reads=reads, writes=writes)

        def proj_fm(col0, M, u, dst_fn):
            p = next_pj()
            for k in range(8):
                P.mm(p.t[:, :], win.t[:, k, col0:col0 + 128], u.t[:, k, :], start=(k == 0), stop=(k == 7),
                     reads=[winb[k], u.s(k)], writes=[p.b])
            dst_fn(p)

        def proj_tm(col0, N, u, s, dst_fn):
            p = next_pj()
            for k in range(8):
                P.mm(p.t[:, 0:N], u.t[:, k, s * 128:(s + 1) * 128], win.t[:, k, col0:col0 + N], start=(k == 0),
                     stop=(k == 7), reads=[winb[k], u.s(k)], writes=[p.b])
            dst_fn(p)

        tiles = debug.get("tiles", list(range(16)))
        first_own_done = [False]
        prevD = [None]
        pending_back = [None]
        for ti, T in enumerate(tiles):
            own = T >= 12
            to = T - 12
            X = xt[ti % 2]
            u = uT[ti % 2]
            P.dma("sync", X.t[:], x_v[T], writes=[X.b])
            for k in range(8):
                pt = ptr[k % 2]
                for s in range(4):
                    P.tr(pt.t[:, s * 128:(s + 1) * 128], X.t[:, s, k * 128:(k + 1) * 128], ident_f,
                         reads=[X.b, cf.b], writes=[pt.b], inc=(s == 3))
                P.act(u.t[:, k, :], pt.t[:], AF.Identity, reads=[pt.b, mv.b], writes=[u.s(k)],
                      bias=mvc(1, k), scale=mvc(0, k))
            KS = kst[ti % 2]
            VS = vst[ti % 2]

            def k_proj(m_):
                proj_fm(C_FK + m_ * 128, 128, u, lambda p: evac_copy(KS.t[:, m_, :], p.t[:], [p.b], [KS.b]))

            def v_proj(s_):
                proj_tm(C_FV, 512, u, s_,
                        lambda p: evac_copy(VS.t[:, :, s_, 0:64], p.t[:].rearrange("p (h d) -> p h d", d=64),
                                            [p.b], [VS.b]))
            P.dma("sync", isbr.t[:], isbrow_d[:, T * 512:(T + 1) * 512], writes=[isbr.b])
            Dc = Dt[ti % 2]

            def ff_post(p, Dc=Dc):
                P.act(ffe.t[:], p.t[0:8, :], AF.Exp, reads=[p.b, nbf.b], writes=[ffe.b], bias=nbf.t[:, 0:1], scale=-1.0)
                P.act(ffl.t[:], ffe.t[:], AF.Ln, reads=[ffe.b], writes=[ffl.b], bias=1.0)
                P.op("vector", lambda e: e.scalar_tensor_tensor(ffl.t[:], ffl.t[:], -1.0, isbr.t[:], ALU.mult, ALU.mult),
                     reads=[ffl.b, isbr.b], writes=[ffl.b])
                pd = prevD[0]
                init = 0.0 if pd is None else pd.t[0:8, 511:512]
                rds = [ones8.b, ffl.b] + ([] if pd is None else [pd.b])
                P.op("vector", lambda e, init=init: e.tensor_tensor_scan(Dc.t[0:8, :], ones8.t[:], ffl.t[:], init, ALU.mult,
                                                                         ALU.add), reads=rds, writes=[Dc.b])
                prevD[0] = Dc
                P.op("vector", lambda e: e.scalar_tensor_tensor(nd.t[:], isbr.t[:], -NEG, Dc.t[0:8, :], ALU.mult,
                                                                ALU.subtract), reads=[isbr.b, Dc.b], writes=[nd.b])
                P.op("vector", lambda e: e.tensor_scalar_add(nd.t[:], nd.t[:], NEG), reads=[nd.b], writes=[nd.b])
                P.op("vector", lambda e: e.tensor_copy(nq.t[:, 0, :], nd.t[:]), reads=[nd.b], writes=[nq.b])
                P.op("vector", lambda e: e.tensor_tensor(nr.t[:], nd.t[:], nq.t[:, 0, :], ALU.subtract),
                     reads=[nd.b, nq.b], writes=[nr.b])
                P.op("vector", lambda e: e.tensor_copy(nq.t[:, 1, :], nr.t[:]), reads=[nr.b], writes=[nq.b])
                P.op("vector", lambda e: e.tensor_tensor(nr.t[:], nr.t[:], nq.t[:, 1, :], ALU.subtract),
                     reads=[nr.b, nq.b], writes=[nr.b])
                P.op("vector", lambda e: e.tensor_copy(nq.t[:, 2, :], nr.t[:]), reads=[nr.b], writes=[nq.b])
                P.dma("sync", NDscr[:, :, T * 512:(T + 1) * 512], nq.t[:], reads=[nq.b], writes=[bNDscr])

            proj_fm(C_GA, 16, u, lambda p: evac_copy(gaa.t[0:16, :], p.t[0:16, :], [p.b], [gaa.b]))
            proj_fm(C_FF, 8, u, ff_post)
            if own and "dq" not in skip:
                P.op("vector", lambda e, Dc=Dc: e.tensor_copy(dq.t[:, 0, :], Dc.t[0:8, :]), reads=[Dc.b], writes=[dq.b])
                P.op("vector", lambda e, Dc=Dc: e.tensor_tensor(dr.t[:], Dc.t[0:8, :], dq.t[:, 0, :], ALU.subtract),
                     reads=[Dc.b, dq.b], writes=[dr.b])
                P.op("vector", lambda e: e.tensor_copy(dq.t[:, 1, :], dr.t[:]), reads=[dr.b], writes=[dq.b])
                P.op("vector", lambda e: e.tensor_tensor(dr.t[:], dr.t[:], dq.t[:, 1, :], ALU.subtract),
                     reads=[dr.b, dq.b], writes=[dr.b])
                P.op("vector", lambda e: e.tensor_copy(dq.t[:, 2, :], dr.t[:]), reads=[dr.b], writes=[dq.b])
                P.dma("sync", Qscr[:, 64:67, to * 512:(to + 1) * 512], dq.t[:], reads=[dq.b], writes=[bQscr])
            if own and "fq" not in skip:
                for m_ in range(4):
                    proj_fm(C_FQ + m_ * 128, 128, u,
                            lambda p, m_=m_: P.act(qst.t[:, m_, :], p.t[:], AF.Copy, reads=[p.b], writes=[qst.b],
                                                   scale=0.125))
                for two in range(2):
                    P.dma("sync", Qscr_v[two][:, :, to * 512:(to + 1) * 512], qst.t[two * 64:(two + 1) * 64, :, :],
                          reads=[qst.b], writes=[bQscr])

            gown = own and "gla" not in skip
            if gown:
                for pr in range(2):
                    pz = next_pj()
                    P.mm(pz.t[:], wa2.t[:, pr * 128:(pr + 1) * 128], gaa.t[:, :], True, True,
                         reads=[wa2.b, gaa.b], writes=[pz.b])
                    P.act(lzT.t[:], pz.t[:], AF.Exp, reads=[pz.b], writes=[lzT.b], scale=-1.0)
                    P.act(lzT.t[:], lzT.t[:], AF.Ln, reads=[lzT.b], writes=[lzT.b], bias=1.0)
                    P.op("vector", lambda e: e.tensor_tensor_scan(bTp.t[:], rmask.t[:], lzT.t[:], 0.0, ALU.mult, ALU.add),
                         reads=[rmask.b, lzT.b], writes=[bTp.b])
                    P.act(ebn.t[:, 0, :], bTp.t[:], AF.Exp, reads=[bTp.b], writes=[ebn.s(0)], scale=-1.0 / 16)
                    P.act(ebn.t[:, 1, :], bTp.t[:], AF.Exp, reads=[bTp.b], writes=[ebn.s(1)], scale=1.0 / 16)
                    def gq_post(p, pr=pr):
                        for hh in range(2):
                            P.op("vector", lambda e, hh=hh: e.scalar_tensor_tensor(
                                qd[2 * pr + hh].t[:], p.t[:], hm[:, hh:hh + 1], ebn.t[:, 0, :], ALU.mult, ALU.mult),
                                 reads=[p.b, ebn.s(0), cf.b], writes=[qd[2 * pr + hh].b])

                    proj_fm(C_GQ + pr * 128, 128, u, gq_post)
                    proj_fm(C_GK + pr * 128, 128, u, lambda p, pr=pr: P.op(
                        "vector", lambda e: e.tensor_tensor(ki[pr].t[:], p.t[:], ebn.t[:, 1, :], ALU.mult),
                        reads=[p.b, ebn.s(1)], writes=[ki[pr].b]))
                for h in range(4):
                    proj_fm(C_GR + h * 128, 128, u,
                            lambda p, h=h: P.act(sr.t[:, h, :], p.t[:], SILU, reads=[p.b], writes=[sr.s(h)]))
                if not first_own_done[0]:
                    first_own_done[0] = True
                    for pr in range(2):
                        P.op("vector", lambda e, pr=pr: e.tensor_copy(Sb.t[:, pr, :], S.t[:, pr, :]),
                             reads=[S.s(pr)], writes=[Sb.s(pr)])

            for s in range(4):
                kt = 4 * T + s
                i2 = s % 2
                P.mm(pG0.t[:, 0:256], gaa.t[:, s * 128:(s + 1) * 128], wa2.t[:, :], True, True,
                     reads=[gaa.b, wa2.b], writes=[pG0.s(0)])
                k_proj(s)
                P.act(t1[i2].t[:], pG0.t[:, 0:256], AF.Exp, reads=[pG0.s(0)], writes=[t1[i2].b], scale=-1.0)
                P.act(t1[i2].t[:], t1[i2].t[:], AF.Ln, reads=[t1[i2].b], writes=[t1[i2].b], bias=1.0)
                P.op("vector", lambda e, i2=i2, kt=kt: e.tensor_scalar(la[i2].t[:], t1[i2].t[:], nisb16.t[:, kt:kt + 1],
                                                                        None, ALU.mult),
                     reads=[t1[i2].b, nisb16.b], writes=[la[i2].b])
                P.mm(pG0.t[:, 256:512], U_f, la[i2].t[:], True, True, reads=[cf.b, la[i2].b], writes=[pG0.s(1)])
                v_proj(s)
                P.act(ee[i2].t[:], pG0.t[:, 256:512], AF.Exp, reads=[pG0.s(1)], writes=[ee[i2].b])
                def gk_post(p, i2=i2, kt=kt):
                    for half in range(2):
                        P.op("vector", lambda e, half=half: e.scalar_tensor_tensor(
                            ke[i2].t[:, half, :], p.t[:, 0:256], isbh.t[:, half, kt:kt + 1], ee[i2].t[:], ALU.mult,
                            ALU.mult), reads=[p.b, isbh.b, ee[i2].b], writes=[ke[i2].b])

                proj_tm(C_GK, 256, u, s, gk_post)
                proj_tm(C_GV, 512, u, s, lambda p, i2=i2: evac_copy(gvb[i2].t[:], p.t[:], [p.b], [gvb[i2].b]))
                for pr in range(2):
                    P.mm(pG2.t[:, pr * 8 + 2 * s:pr * 8 + 2 * s + 2], la[i2].t[:, pr * 128:(pr + 1) * 128], chunkind,
                         True, True, reads=[la[i2].b, cf.b], writes=[pG2.s("bl")])
                P.act(dec.t[:, :, 2 * s:2 * s + 2],
                      pG2.t[:, 0:16].rearrange("p (r c) -> p r c", c=8)[:, :, 2 * s:2 * s + 2], AF.Exp,
                      reads=[pG2.s("bl")], writes=[dec.b])
                def back(s=s, i2=i2, kt=kt, gown=gown):
                    if gown:
                        for pr in range(2):
                            for hh in range(2):
                                P.mm(pG2.t[:, 256 + hh * 128:256 + (hh + 1) * 128],
                                     ki[pr].t[:, s * 128:(s + 1) * 128],
                                     qd[2 * pr + hh].t[:, s * 128:(s + 1) * 128], True, True,
                                     reads=[ki[pr].b, qd[2 * pr + hh].b], writes=[pG2.s("A")], inc=(hh == 1))
                            P.op("vector", lambda e, pr=pr: e.tensor_tensor(
                                At[pr].t[:], pG2.t[:, 256:512].rearrange("p (h t) -> p h t", t=128), M01x2, ALU.mult),
                                 reads=[pG2.s("A"), cb.b], writes=[At[pr].b])
                    for half in range(2):
                        c = 2 * s + half
                        if gown:
                            for pr in range(2):
                                for hh in range(2):
                                    h = 2 * pr + hh
                                    oc = pG3.t[:, h * 128 + half * 64:h * 128 + half * 64 + 64]
                                    P.mm(oc, gvb[i2].t[:, h * 128:(h + 1) * 128], At[pr].t[:, hh, half * 64:half * 64 + 64],
                                         True, False, reads=[gvb[i2].b, At[pr].b], writes=[pG3.b], inc=False)
                                    P.mm(oc, Sb.t[:, pr, hh * 128:(hh + 1) * 128],
                                         qd[h].t[:, s * 128 + half * 64:s * 128 + half * 64 + 64],
                                         False, True, reads=[Sb.s(pr), qd[h].b], writes=[pG3.b], inc=True)
                        for pr in range(2):
                            P.mm(pG1.t[:, pr * 256:(pr + 1) * 256],
                                 ke[i2].t[:, half, pr * 128:(pr + 1) * 128],
                                 gvb[i2].t[:, pr * 256:(pr + 1) * 256], True, True,
                                 reads=[ke[i2].b, gvb[i2].b], writes=[pG1.s(pr)])
                            P.op("vector", lambda e, pr=pr, c=c: e.scalar_tensor_tensor(
                                S.t[:, pr, :], S.t[:, pr, :], dec.t[:, pr, c:c + 1], pG1.t[:, pr * 256:(pr + 1) * 256],
                                ALU.mult, ALU.add), reads=[S.s(pr), dec.b, pG1.s(pr)], writes=[S.s(pr)])
                            if gown:
                                P.op(POOL, lambda e, pr=pr: e.tensor_copy(Sb.t[:, pr, :], S.t[:, pr, :]),
                                     reads=[S.s(pr)], writes=[Sb.s(pr)])
                    if gown:
                        evac_copy(oT.t[:, :, s * 128:(s + 1) * 128], pG3.t[:].rearrange("p (h t) -> p h t", t=128),
                                  [pG3.b], [oT.b])

                if pending_back[0] is not None:
                    pending_back[0]()
                pending_back[0] = back
            if pending_back[0] is not None:
                pending_back[0]()
                pending_back[0] = None
            P.dma("sync", Kscr_v[:, :, T * 512:(T + 1) * 512], KS.t[:], reads=[KS.b], writes=[bKscr])
            P.dma("scalar", Vscr_v[:, :, 4 * T:4 * T + 4, :], VS.t[:], reads=[VS.b], writes=[bVscr])
            if gown and "gla_n" not in skip:
                for h in range(4):
                    P.op(POOL, lambda e, h=h: e.tensor_tensor(sq.t[:], oT.t[:, h, :], oT.t[:, h, :], ALU.mult),
                         reads=[oT.b], writes=[sq.b])
                    pm = next_pj()
                    P.mm(pm.t[:], ones_f, sq.t[:], True, True, reads=[cf.b, sq.b], writes=[pm.b])
                    if "gdump" in skip and h == 3:
                        dump("pm", pm.t[:], [128, 512], F32, [pm.b]) if False else None
                    if "fbias" in skip:
                        P.act(rstd.t[:], pm.t[:], AF.Ln, reads=[pm.b], writes=[rstd.b], bias=RMS_EPS, scale=1.0 / 128)
                    else:
                        P.act(rstd.t[:], pm.t[:], AF.Ln, reads=[pm.b, cf.b], writes=[rstd.b], bias=cf.t[:, 4, 2:3],
                              scale=1.0 / 128)
                    if "gdump" in skip and h == 3:
                        dump("lnv", rstd.t[:], [128, 512], F32, [rstd.b])
                    P.act(rstd.t[:], rstd.t[:], AF.Exp, reads=[rstd.b], writes=[rstd.b], scale=-0.5)
                    P.op("vector", lambda e, h=h: e.tensor_tensor(go.t[:], oT.t[:, h, :], rstd.t[:], ALU.mult),
                         reads=[oT.b, rstd.b], writes=[go.b])
                    P.op("vector", lambda e, h=h: e.scalar_tensor_tensor(
                        attnT.t[:, 4 + h, to * 512:(to + 1) * 512], go.t[:], ggla.t[:, h:h + 1], sr.t[:, h, :], ALU.mult,
                        ALU.mult), reads=[go.b, ggla.b, sr.s(h)], writes=[attnT.s(4 + h)])
        if stop_after == "pA" and "gdump" in skip:
            dump("oT", oT.t[:], [128, 4, 512], F32, [oT.b])
            dump("sr", sr.t[:], [128, 4, 512], BF16, [sr.s(h) for h in range(4)])
            dump("rstd", rstd.t[:], [128, 512], F32, [rstd.b])
            dump("go", go.t[:], [128, 512], F32, [go.b])
            dump("sq", sq.t[:], [128, 512], F32, [sq.b])
        if stop_after == "pA":
            dump("attnT", attnT.t[:], [128, 8, OWN], BF16, [attnT.s(4 + h) for h in range(4)])
            dump("negDk", negDk.t[:], [128, 64, 8], F32, [negDk.b])
            dump("S", S.t[:], [128, 2, 256], F32, [S.s(0), S.s(1)])
            nt = len(tiles)
            if nt == 16:
                dump("K", Kscr, [512, SEQ], BF16, [bKscr])
                dump("V", Vscr, [8, 128, 64, 65], BF16, [bVscr])
                dump("Q", Qscr, [8, 67, OWN], BF16, [bQscr])
            else:
                T0 = tiles[0]
                dump("K", Kscr[:, T0 * 512:(T0 + 1) * 512], [512, 512], BF16, [bKscr])
                dump("V", Vscr[:, :, 4 * T0:4 * T0 + 4, :], [8, 128, 4, 65], BF16, [bVscr])
                if tiles[-1] >= 12:
                    to_ = tiles[-1] - 12
                    dump("Q", Qscr[:, :, to_ * 512:(to_ + 1) * 512], [8, 67, 512], BF16, [bQscr])
        finish()
    win_scope.close()
    if stop_after == "pA":
        attn_scope.close()
        outer.close()
        return nc, dbg_outs
    wo_scope = ExitStack()
    if not early:
        wo = sb(wo_scope, "wo", [128, 8, D], BF16)
        for k in range(8):
            P.dma("gpsimd", wo.t[:, k, :], w_o[k * 128:(k + 1) * 128, :], writes=[wo.s(k)], max_dma_last_dim=4096)

    with ExitStack() as ph:
        kq = [sb(ph, "kq%d" % i, [70, SEQ], BF16) for i in range(2)]
        vq = [sb(ph, "vq%d" % i, [128, 64, 65], BF16) for i in range(2)]
        qq = [sb(ph, "qq%d" % i, [70, OWN], BF16) for i in range(2)]
        Pt = [sb(ph, "Pt%d" % i, [128, 512], BF16) for i in range(3)]
        Osb = sb(ph, "Osb", [65, 512], F32)
        R = sb(ph, "R", [128, 512], F32)
        lnr = sb(ph, "lnr", [65, 512], F32)
        On = sb(ph, "On", [128, 512], BF16)
        Sp = [ps(ph, "Sp%d" % i) for i in range(3)]
        Op = [ps(ph, "Op%d" % i) for i in range(2)]
        bcp = ps(ph, "bcp")
        shp = ps(ph, "shp")
        P.op("gpsimd", lambda e: e.memset(R.t[:], 0.0), writes=[R.b])
        P.op("gpsimd", lambda e: e.memset(On.t[:], 0.0), writes=[On.b])
        for i in range(2):
            P.op("gpsimd", lambda e, i=i: e.memset(kq[i].t[64:70, :], 1.0), writes=[kq[i].s("ones"), kq[i].b])
            P.op("gpsimd", lambda e, i=i: e.memset(qq[i].t[64:70, :], 1.0), writes=[qq[i].b])
        heads = debug.get("heads", list(range(8)))
        it = 0
        for hi, h in enumerate(heads):
            KQ, VQ, QQ = kq[hi % 2], vq[hi % 2], qq[hi % 2]
            P.dma("sync", KQ.t[0:64, :], Kscr[h * 64:(h + 1) * 64, :], reads=[bKscr], writes=[KQ.b])
            P.dma("scalar", VQ.t[:], Vscr[h], reads=[bVscr], writes=[VQ.b])
            P.dma("sync", QQ.t[0:67, :], Qscr[h], reads=[bQscr], writes=[QQ.b])
            P.dma("sync", KQ.t[67:70, :], NDscr[h], reads=[bNDscr], writes=[KQ.b])
            for qt in range(4):
                O = Op[qt % 2]
                order = [(0, 0, False)]
                for jj in range(4):
                    order.append((48 + 4 * qt + jj, jj * 128, True))
                for j in range(4 * qt):
                    order.append((48 + j, 0, False))
                for kt in range(1, 48):
                    order.append((kt, 0, False))
                n = len(order)

                def qk(i):
                    kt, c0, diag = order[i]
                    SP, PT = Sp[(it + i) % 3], Pt[(it + i) % 3]
                    P.mm(SP.t[:, c0:512], KQ.t[0:70, kt * 128:(kt + 1) * 128], QQ.t[0:70, qt * 512 + c0:(qt + 1) * 512],
                         True, not diag, reads=[KQ.b, KQ.s("ones"), QQ.b], writes=[SP.b], inc=(not diag))
                    if diag:
                        P.mm(SP.t[:, c0:c0 + 128], ident_b, trimask, False, True, reads=[cb.b], writes=[SP.b], inc=True)
                    P.act(PT.t[:, c0:512], SP.t[:, c0:512], AF.Exp, reads=[SP.b], writes=[PT.b])

                def pv(i):
                    kt, c0, diag = order[i]
                    PT = Pt[(it + i) % 3]
                    P.mm(O.t[0:65, c0:512], VQ.t[:, kt, 0:65], PT.t[:, c0:512], i == 0, i == n - 1,
                         reads=[VQ.b, PT.b], writes=[O.b], inc=(i == n - 1))

                LOOK = 2
                for i in range(n + LOOK):
                    if i < n:
                        qk(i)
                    if i >= LOOK:
                        pv(i - LOOK)
                it += n
                P.op("vector", lambda e, O=O: e.tensor_copy(Osb.t[:], O.t[0:65, :]), reads=[O.b], writes=[Osb.b])
                P.act(lnr.t[64:65, :], Osb.t[64:65, :], AF.Ln, reads=[Osb.b], writes=[lnr.b])
                P.act(R.t[64:65, :], lnr.t[64:65, :], AF.Exp, reads=[lnr.b], writes=[R.b], scale=-1.0)
                P.mm(bcp.t[:, :], E64_f, R.t[:], True, True, reads=[cf.b, R.b], writes=[bcp.b])
                cols = slice(qt * 512, (qt + 1) * 512)
                if h % 2 == 0:
                    P.op("vector", lambda e, h=h, cols=cols: e.tensor_tensor(attnT.t[0:64, h // 2, cols], Osb.t[0:64, :],
                                                                              bcp.t[0:64, :], ALU.mult),
                         reads=[Osb.b, bcp.b], writes=[attnT.s(h // 2)])
                else:
                    P.op("vector", lambda e: e.tensor_tensor(On.t[0:64, :], Osb.t[0:64, :], bcp.t[0:64, :], ALU.mult),
                         reads=[Osb.b, bcp.b], writes=[On.b])
                    P.mm(shp.t[:], shiftup, On.t[:], True, True, reads=[cb.b, On.b], writes=[shp.b])
                    P.op("vector", lambda e, h=h, cols=cols: e.tensor_copy(attnT.t[64:128, h // 2, cols],
                                                                            shp.t[64:128, :]),
                         reads=[shp.b], writes=[attnT.s(h // 2)])
        if stop_after == "pF":
            dump("attnT", attnT.t[:], [128, 8, OWN], BF16, [attnT.s(i) for i in range(8)])
        finish()
    if stop_after == "pF":
        wo_scope.close()
        attn_scope.close()
        outer.close()
        return nc, dbg_outs

    x_own = x_loc[NREST:SEQ, :].rearrange("(T s p) f -> T p s f", p=128, s=4)

    def layer_norm_stats(stk_tiles, z, n_out_fn):
        sqt, mean_sb, msq, var, pA, pB = stk_tiles
        for oc in range(8):
            P.mm(pA.t[:], ones_f, z.t[:, oc, :], oc == 0, oc == 7, reads=[cf.b, z.s(oc)], writes=[pA.b])
        for oc in range(8):
            P.op(POOL, lambda e, oc=oc: e.tensor_tensor(sqt.t[:, oc % 2, :], z.t[:, oc, :], z.t[:, oc, :], ALU.mult),
                 reads=[z.s(oc)], writes=[sqt.s(oc % 2)])
            P.mm(pB.t[:], ones_f, sqt.t[:, oc % 2, :], oc == 0, oc == 7, reads=[cf.b, sqt.s(oc % 2)], writes=[pB.b],
                 inc=True)
        P.op("vector", lambda e: e.tensor_scalar_mul(mean_sb.t[:], pA.t[:], 1.0 / D), reads=[pA.b], writes=[mean_sb.b])
        P.op("vector", lambda e: e.tensor_tensor(msq.t[:], mean_sb.t[:], mean_sb.t[:], ALU.mult), reads=[mean_sb.b],
             writes=[msq.b])
        P.op("vector", lambda e: e.scalar_tensor_tensor(var.t[:], pB.t[:], 1.0 / D, msq.t[:], ALU.mult, ALU.subtract),
             reads=[pB.b, msq.b], writes=[var.b])
        P.act(var.t[:], var.t[:], AF.Ln, reads=[var.b, cf.b], writes=[var.b], bias=cf.t[:, 4, 3:4])
        P.act(var.t[:], var.t[:], AF.Exp, reads=[var.b], writes=[var.b], scale=-0.5)
        for oc in range(8):
            P.op("vector", lambda e, oc=oc: e.tensor_tensor(z.t[:, oc, :], z.t[:, oc, :], mean_sb.t[:], ALU.subtract),
                 reads=[z.s(oc), mean_sb.b], writes=[z.s(oc)])
            P.op(POOL, lambda e, oc=oc: e.tensor_tensor(z.t[:, oc, :], z.t[:, oc, :], var.t[:], ALU.mult),
                 reads=[z.s(oc), var.b], writes=[z.s(oc)])
            n_out_fn(oc)

    with ExitStack() as ph:
        xt = [sb(ph, "xw%d" % i, [128, 4, D], F32) for i in range(2)]
        xz = [sb(ph, "xz%d" % i, [128, 8, 512], F32) for i in range(2)]
        sqt = sb(ph, "sqt", [128, 2, 512], F32)
        mean_sb = sb(ph, "mean", [128, 512], F32)
        msq = sb(ph, "msq", [128, 512], F32)
        var = sb(ph, "var", [128, 512], F32)
        u2s = [sb(ph, "u2s%d" % i, [128, 8, 512], BF16) for i in range(2)]
        x1s = [sb(ph, "x1s%d" % i, [128, 8, 512], F32) for i in range(2)]
        ptr = [ps(ph, "wtr%d" % i) for i in range(2)]
        pj = [ps(ph, "wpj%d" % i) for i in range(2)]
        pA = ps(ph, "wpA")
        pB = ps(ph, "wpB")

        def stageW_A(to):
            X = xt[to % 2]
            Z = xz[to % 2]
            cols = slice(to * 512, (to + 1) * 512)
            P.dma("sync", X.t[:], x_own[to], writes=[X.b])
            for k in range(8):
                pt = ptr[k % 2]
                for s in range(4):
                    P.tr(pt.t[:, s * 128:(s + 1) * 128], X.t[:, s, k * 128:(k + 1) * 128], ident_f, reads=[X.b, cf.b],
                         writes=[pt.b], inc=(s == 3))
                P.act(Z.t[:, k, :], pt.t[:], AF.Copy, reads=[pt.b], writes=[Z.s(k)], scale=ALPHA)
            for oc in range(8):
                p = pj[oc % 2]
                for kc in range(8):
                    P.mm(p.t[:], wo.t[:, kc, oc * 128:(oc + 1) * 128], attnT.t[:, kc, cols], kc == 0, kc == 7,
                         reads=[wo.s(kc), attnT.s(kc)], writes=[p.b])
                P.op("vector", lambda e, oc=oc, p=p: e.scalar_tensor_tensor(Z.t[:, oc, :], p.t[:], mvc(2, oc),
                                                                            Z.t[:, oc, :], ALU.mult, ALU.add),
                     reads=[p.b, mv.b, Z.s(oc)], writes=[Z.s(oc)])

        def stageW_B(to):
            Z = xz[to % 2]
            cols = slice(to * 512, (to + 1) * 512)
            U2, X1 = u2s[to % 2], x1s[to % 2]

            def n_out(oc):
                P.act(U2.t[:, oc, :], Z.t[:, oc, :], AF.Identity, reads=[Z.s(oc), mv.b], writes=[U2.b],
                      bias=mvc(4, oc), scale=mvc(3, oc))
                P.act(X1.t[:, oc, :], Z.t[:, oc, :], AF.Identity, reads=[Z.s(oc), mv.b], writes=[X1.b],
                      bias=mvc(6, oc), scale=mvc(5, oc))

            layer_norm_stats((sqt, mean_sb, msq, var, pA, pB), Z, n_out)
            P.dma("sync", U2scr[:, :, cols], U2.t[:], reads=[U2.b], writes=[bU2scr])
            P.dma("scalar", X1scr[:, :, cols], X1.t[:], reads=[X1.b], writes=[bX1scr])

        stageW_A(0)
        for to in range(4):
            if to + 1 < 4:
                stageW_A(to + 1)
            stageW_B(to)
        if stop_after == "pW":
            dump("U2", U2scr, [128, 8, OWN], BF16, [bU2scr])
            dump("X1", X1scr, [128, 8, OWN], F32, [bX1scr])
        finish()
    wo_scope.close()
    attn_scope.close()
    if stop_after == "pW":
        outer.close()
        return nc, dbg_outs

    with ExitStack() as hs:
        hT = sb(hs, "hT", [128, NFC, OWN], BF16)
        wd = sb(hs, "wd", [128, NFC, D], BF16)
        with ExitStack() as ph:
            u2 = sb(ph, "u2", [128, 8, OWN], BF16)
            wg = [sb(ph, "wg%d" % i, [128, 8, 256], BF16) for i in range(2)]
            wu = [sb(ph, "wu%d" % i, [128, 8, 256], BF16) for i in range(2)]
            sg = [sb(ph, "sg%d" % i, [128, 512], F32) for i in range(2)]
            pg = [ps(ph, "pg%d" % i) for i in range(2)]
            pu = [ps(ph, "pu%d" % i) for i in range(2)]
            P.dma("sync", u2.t[:], U2scr, reads=[bU2scr], writes=[u2.b])
            for fc in range(NFC):
                if fc == 1:
                    pass
            wd_pending = list(range(NFC))
            wg_v = w_gate.rearrange("(k p) n -> p k n", p=128)
            wu_v = w_up.rearrange("(k p) n -> p k n", p=128)
            cnt = 0
            for c2 in range(NFC // 2):
                WG, WU = wg[c2 % 2], wu[c2 % 2]
                P.dma("gpsimd", WG.t[:], wg_v[:, :, c2 * 256:(c2 + 1) * 256], writes=[WG.b])
                P.dma("gpsimd", WU.t[:], wu_v[:, :, c2 * 256:(c2 + 1) * 256], writes=[WU.b])
                for _ in range(2):
                    if wd_pending:
                        fc_ = wd_pending.pop(0)
                        P.dma("gpsimd", wd.t[:, fc_, :], w_down[fc_ * 128:(fc_ + 1) * 128, :], writes=[wd.s(fc_)],
                              max_dma_last_dim=4096)
                for f2 in range(2):
                    fc = c2 * 2 + f2
                    for to in range(4):
                        cols = slice(to * 512, (to + 1) * 512)
                        G, Uu, SG = pg[cnt % 2], pu[cnt % 2], sg[cnt % 2]
                        cnt += 1
                        for k in range(8):
                            P.mm(G.t[:], WG.t[:, k, f2 * 128:(f2 + 1) * 128], u2.t[:, k, cols], k == 0, k == 7,
                                 reads=[WG.b, u2.b], writes=[G.b])
                        for k in range(8):
                            P.mm(Uu.t[:], WU.t[:, k, f2 * 128:(f2 + 1) * 128], u2.t[:, k, cols], k == 0, k == 7,
                                 reads=[WU.b, u2.b], writes=[Uu.b])
                        P.act(SG.t[:], G.t[:], SILU, reads=[G.b], writes=[SG.b])
                        P.op("vector", lambda e, fc=fc, cols=cols, SG=SG, Uu=Uu: e.tensor_tensor(
                            hT.t[:, fc, cols], SG.t[:], Uu.t[:], ALU.mult), reads=[SG.b, Uu.b], writes=[hT.s(fc)])
            finish()
        with ExitStack() as ph:
            x1z = [sb(ph, "x1z%d" % i, [128, 8, 512], F32) for i in range(2)]
            sqt = sb(ph, "sqt2", [128, 2, 512], F32)
            mean_sb = sb(ph, "mean2", [128, 512], F32)
            msq = sb(ph, "msq2", [128, 512], F32)
            var = sb(ph, "var2", [128, 512], F32)
            ot = [sb(ph, "ot%d" % i, [128, D], F32) for i in range(2)]
            pj = [ps(ph, "fpj%d" % i) for i in range(2)]
            pA = ps(ph, "fpA")
            pB = ps(ph, "fpB")
            ptr = [ps(ph, "ftr%d" % i) for i in range(2)]

            def stage2_A(to):
                cols = slice(to * 512, (to + 1) * 512)
                Z = x1z[to % 2]
                P.dma("sync", Z.t[:], X1scr[:, :, cols], reads=[bX1scr], writes=[Z.s(oc) for oc in range(8)])
                for oc in range(8):
                    p = pj[oc % 2]
                    for fc in range(NFC):
                        P.mm(p.t[:], wd.t[:, fc, oc * 128:(oc + 1) * 128], hT.t[:, fc, cols], fc == 0, fc == NFC - 1,
                             reads=[wd.s(fc), hT.s(fc)], writes=[p.b])
                    P.op("vector", lambda e, oc=oc, p=p: e.scalar_tensor_tensor(
                        Z.t[:, oc, :], p.t[:], mvc(7, oc), Z.t[:, oc, :], ALU.mult, ALU.add),
                         reads=[p.b, mv.b, Z.s(oc)], writes=[Z.s(oc)])

            def stage2_B(to):
                Z = x1z[to % 2]

                def n_out2(oc):
                    P.act(Z.t[:, oc, :], Z.t[:, oc, :], AF.Identity, reads=[Z.s(oc), lnp.b], writes=[Z.s(oc)],
                          bias=lnp.t[:, 24 + oc:25 + oc], scale=lnp.t[:, 16 + oc:17 + oc])

                layer_norm_stats((sqt, mean_sb, msq, var, pA, pB), Z, n_out2)
                for s in range(4):
                    OT = ot[s % 2]
                    for half in range(2):
                        pt = ptr[half]
                        for q4 in range(4):
                            oc = half * 4 + q4
                            P.tr(pt.t[:, q4 * 128:(q4 + 1) * 128], Z.t[:, oc, s * 128:(s + 1) * 128], ident_f,
                                 reads=[Z.s(oc), cf.b], writes=[pt.b], inc=(q4 == 3))
                        if half == 0:
                            P.op("vector", lambda e, OT=OT, pt=pt: e.tensor_copy(OT.t[:, 0:512], pt.t[:]),
                                 reads=[pt.b], writes=[OT.s(0)])
                        else:
                            P.act(OT.t[:, 512:1024], pt.t[:], AF.Copy, reads=[pt.b], writes=[OT.s(1)])
                    r0 = to * 512 + s * 128
                    P.dma("sync", out_d[r0:r0 + 128, :], OT.t[:], reads=[OT.s(0), OT.s(1)], writes=[bout])

            stage2_A(0)
            for to in range(4):
                if to + 1 < 4:
                    stage2_A(to + 1)
                stage2_B(to)
            finish()
    outer.close()
    return nc, dbg_outs


def _consts():
    idx = np.arange(128)
    s_, t_ = idx[:, None], idx[None, :]
    same = (s_ // 64) == (t_ // 64)
    cf = np.zeros((128, 5, 128), np.float32)
    cf[:, 0, :] = np.eye(128, dtype=np.float32)
    cf[:, 1, :] = ((s_ > t_) & same).astype(np.float32)
    cf[:, 2, :] = 1.0
    cf[64, 3, 0:64] = 1.0
    cf[:, 4, 0] = (idx < 64)
    cf[:, 4, 1] = (idx >= 64)
    cf[:, 4, 2] = RMS_EPS
    cf[:, 4, 3] = LN_EPS
    cf[:, 4, 4] = np.where(idx < 64, 0.125, 0.0)
    cf[:, 4, 5] = np.where(idx >= 64, 0.125, 0.0)
    cb = np.zeros((128, 5, 128), np.float32)
    cb[:, 0, :] = np.eye(128, dtype=np.float32)
    cb[:, 1, :] = np.where(s_ > t_, NEG, 0.0)
    cb[:, 2, :] = ((s_ <= t_) & same).astype(np.float32)
    cb[:, 3, :] = cb[:, 2, :]
    cb[0:64, 4, 64:128] = np.eye(64, dtype=np.float32)
    rmask = np.ones((128, 512), np.float32)
    rmask[:, ::64] = 0.0
    return cf, cb.astype(ml_dtypes.bfloat16), rmask


def make_in_maps(x, c, w_c, b_c, w_in, b_f, w_a2, b_a, g_gla, w_o, ln1_g, ln1_b, w_gate, w_up, w_down, ln2_g, ln2_b):
    f = lambda a: np.ascontiguousarray(np.asarray(a, dtype=np.float32))
    x, c, w_c, b_c, w_in = f(x), f(c), f(w_c), f(b_c), f(w_in)
    cf, cb, rmask = _consts()
    pp = lambda v: np.ascontiguousarray(f(v).reshape(-1, 128).T)
    shared = {
        "w_c": w_c, "b_cT": pp(b_c), "w_in": w_in, "nbf": np.ascontiguousarray(-f(b_f).reshape(8, 1)),
        "wa2": np.ascontiguousarray(np.concatenate([f(w_a2), f(b_a)[None, :]], axis=0)),
        "ggla": pp(g_gla), "w_o": f(w_o),
        "lnp": np.ascontiguousarray(np.concatenate([pp(ln1_g), pp(ln1_b), pp(ln2_g), pp(ln2_b)], axis=1)),
        "w_gate": f(w_gate), "w_up": f(w_up), "w_down": f(w_down), "cf": cf, "cb": cb, "rmask": rmask,
    }
    in_maps = []
    for core in range(8):
        b, j = core // 4, core % 4
        own = np.arange(OWN * j, OWN * (j + 1))
        rest = np.concatenate([np.arange(0, OWN * j), np.arange(OWN * (j + 1), SEQ)])
        loc = np.concatenate([rest, own])
        isb_loc = np.concatenate([(rest < OWN * j), np.ones(OWN, bool)]).astype(np.float32)
        m = dict(shared)
        m["x_loc"] = np.ascontiguousarray(x[b][loc])
        m["cT"] = pp(c[b])
        m["isb"] = np.ascontiguousarray(isb_loc.reshape(64, 128).T)
        ih = np.zeros((128, 2, 64), np.float32)
        ih[0:64, 0, :] = m["isb"][0:64, :]
        ih[64:128, 1, :] = m["isb"][64:128, :]
        m["isbh"] = ih
        m["nisb16"] = np.ascontiguousarray((-isb_loc / 16.0).astype(np.float32).reshape(64, 128).T)
        m["maskb"] = np.ascontiguousarray(((1.0 - isb_loc) * NEG).astype(np.float32).reshape(64, 128).T)
        m["isbrow"] = np.ascontiguousarray(np.broadcast_to(isb_loc[None, :], (8, SEQ)))
        in_maps.append(m)
    return in_maps


def kernel(**inputs):
    nc, dbgo = build(_DEBUG)
    in_maps = make_in_maps(**inputs)
    names = set(dbgo["__inputs__"])
    in_maps = [{k: v for k, v in m.items() if k in names} for m in in_maps]
    res = run_bass_kernel_spmd(nc, in_maps, core_ids=list(range(8)))
    if _DEBUG:
        return res
    out = np.empty((2, SEQ, D), np.float32)
    for core in range(8):
        b, j = core // 4, core % 4
        out[b, OWN * j:OWN * (j + 1), :] = res.results[core]["out"]
    return out
```
